# Optimizing a Trainium2 kernel written in Bass

```python
import jax, jax.numpy as jnp
from jax import lax
import numpy as np

D_MODEL = 1024
BATCH = 8
SEQ = 8192
DEPTH = 2
DEC_BATCH = 16
DEC_SEQ = 2048
PAST_LEN = 128

HEAD_DIM = 64
A_Q_HEADS = 16
A_KV_HEADS = 4
A_RADIUS = 128
B_GROUPS = ((128, 1), (512, 4), (2048, 16))
B_Q_PER_GROUP = 6
B_KV_PER_GROUP = 2
B_Q_HEADS = B_Q_PER_GROUP * len(B_GROUPS)
B_KV_HEADS = B_KV_PER_GROUP * len(B_GROUPS)
FFN_HIDDEN = -(-8 * D_MODEL // (3 * 256)) * 256
PLE_DIM = 256
N_A_LAYERS = (DEPTH + 1) // 2
N_B_LAYERS = DEPTH // 2
EPS = 1e-6
NEG_INF = -1e30

kernel_name = "hybrid_window_dilated_encoder"


def alibi_slopes(n):
    return 2.0 ** (-8.0 * jnp.arange(1, n + 1, dtype=jnp.float32) / n)


def rmsnorm(x, g):
    xf = x.astype(jnp.float32)
    y = xf * lax.rsqrt(jnp.mean(xf * xf, axis=-1, keepdims=True) + EPS) * g.astype(jnp.float32)
    return y.astype(x.dtype)


def banded_attention(q, k, v, slopes, radius, stride, sink):
    n, length, hq, hd = q.shape
    hkv = k.shape[2]
    grp = hq // hkv
    blk = radius
    nb = -(-length // blk)
    lp = nb * blk
    q = jnp.pad(q, ((0, 0), (0, lp - length), (0, 0), (0, 0)))
    kv_pad = ((0, 0), (blk, lp - length + blk), (0, 0), (0, 0))
    k = jnp.pad(k, kv_pad)
    v = jnp.pad(v, kv_pad)
    qb = q.reshape(n, nb, blk, hkv, grp, hd)

    def windows(t):
        tb = t.reshape(n, nb + 2, blk, hkv, hd)
        return jnp.concatenate([tb[:, :-2], tb[:, 1:-1], tb[:, 2:]], axis=2)

    kw, vw = windows(k), windows(v)
    logits = jnp.einsum('nbqkgd,nbskd->nbkgqs', qb, kw, preferred_element_type=jnp.float32)
    qpos = jnp.arange(lp).reshape(nb, blk)
    kpos = (jnp.arange(nb)[:, None] - 1) * blk + jnp.arange(3 * blk)[None, :]
    dist = jnp.abs(qpos[:, :, None] - kpos[:, None, :])
    valid = (dist <= radius) & (kpos[:, None, :] >= 0) & (kpos[:, None, :] < length)
    alibi = -slopes.astype(jnp.float32).reshape(hkv, grp)[None, :, :, None, None] * \
        (stride * dist).astype(jnp.float32)[:, None, None]
    logits = jnp.where(valid[:, None, None], logits + alibi, NEG_INF)
    m = logits.max(axis=-1)
    if sink is not None:
        s = sink.astype(jnp.float32).reshape(hkv, grp)[:, :, None]
        m = jnp.maximum(m, s)
    p = jnp.exp(logits - m[..., None])
    denom = p.sum(axis=-1)
    if sink is not None:
        denom = denom + jnp.exp(s - m)
    o = jnp.einsum('nbkgqs,nbskd->nbqkgd', p.astype(v.dtype), vw, preferred_element_type=jnp.float32)
    o = o / denom.transpose(0, 1, 4, 2, 3)[..., None]
    o = o.reshape(n, lp, hq, hd)[:, :length].astype(q.dtype)
    lse = (m + jnp.log(denom)).transpose(0, 1, 4, 2, 3).reshape(n, lp, hq)[:, :length]
    return o, lse


def split_qkv(h, wqkv, hq, hkv, q_gain, k_gain):
    b, s, _ = h.shape
    qkv = h @ wqkv
    q = qkv[..., :hq * HEAD_DIM].reshape(b, s, hq, HEAD_DIM)
    k = qkv[..., hq * HEAD_DIM:(hq + hkv) * HEAD_DIM].reshape(b, s, hkv, HEAD_DIM)
    v = qkv[..., (hq + hkv) * HEAD_DIM:].reshape(b, s, hkv, HEAD_DIM)
    q = rmsnorm(q, q_gain) * jnp.asarray(HEAD_DIM ** -0.5, dtype=h.dtype)
    k = rmsnorm(k, k_gain)
    return q, k, v


def mixer_window_gqa(h, wqkv, wo, q_gain, k_gain, sink):
    b, s, _ = h.shape
    q, k, v = split_qkv(h, wqkv, A_Q_HEADS, A_KV_HEADS, q_gain, k_gain)
    o, _ = banded_attention(q, k, v, alibi_slopes(A_Q_HEADS), A_RADIUS, 1, sink)
    return o.reshape(b, s, A_Q_HEADS * HEAD_DIM) @ wo


def mixer_dilated(h, wqkv, wo, q_gain, k_gain):
    b, s, _ = h.shape
    q, k, v = split_qkv(h, wqkv, B_Q_HEADS, B_KV_HEADS, q_gain, k_gain)
    slopes = alibi_slopes(B_Q_HEADS)
    outs, lses = [], []
    for g, (window, dil) in enumerate(B_GROUPS):
        radius = window // (2 * dil)
        qs = slice(g * B_Q_PER_GROUP, (g + 1) * B_Q_PER_GROUP)
        ks = slice(g * B_KV_PER_GROUP, (g + 1) * B_KV_PER_GROUP)

        def to_res(t):
            hh = t.shape[2]
            return t.reshape(b, s // dil, dil, hh, HEAD_DIM).transpose(0, 2, 1, 3, 4).reshape(b * dil, s // dil, hh, HEAD_DIM)

        o, lse = banded_attention(to_res(q[:, :, qs]), to_res(k[:, :, ks]), to_res(v[:, :, ks]),
                                  slopes[qs], radius, dil, None)
        o = o.reshape(b, dil, s // dil, B_Q_PER_GROUP, HEAD_DIM).transpose(0, 2, 1, 3, 4).reshape(b, s, B_Q_PER_GROUP, HEAD_DIM)
        lse = lse.reshape(b, dil, s // dil, B_Q_PER_GROUP).transpose(0, 2, 1, 3).reshape(b, s, B_Q_PER_GROUP)
        outs.append(o)
        lses.append(lse)
    alpha = jax.nn.softmax(jnp.stack(lses, axis=0), axis=0)
    o = jnp.concatenate([outs[g] * alpha[g][..., None].astype(h.dtype) for g in range(len(B_GROUPS))], axis=2)
    return o.reshape(b, s, B_Q_HEADS * HEAD_DIM) @ wo


def swiglu(h, w_gate, w_up, w_down):
    return (jax.nn.silu(h @ w_gate) * (h @ w_up)) @ w_down


def encoder_trunk(x, p, norm_mix, norm_ffn, norm_ple,
                  a_wqkv, a_wo, a_q_gain, a_k_gain, a_sink,
                  b_wqkv, b_wo, b_q_gain, b_k_gain,
                  ffn_w_gate, ffn_w_up, ffn_w_down, ple_w_gate, ple_w_proj):
    for i in range(DEPTH):
        hn = rmsnorm(x, norm_mix[i])
        j = i // 2
        if i % 2 == 0:
            mix = mixer_window_gqa(hn, a_wqkv[j], a_wo[j], a_q_gain[j], a_k_gain[j], a_sink[j])
        else:
            mix = mixer_dilated(hn, b_wqkv[j], b_wo[j], b_q_gain[j], b_k_gain[j])
        x = x + mix
        x = x + swiglu(rmsnorm(x, norm_ffn[i]), ffn_w_gate[i], ffn_w_up[i], ffn_w_down[i])
        gate = jax.nn.sigmoid(rmsnorm(x, norm_ple[i]) @ ple_w_gate[i])
        x = x + gate * (p[i] @ ple_w_proj[i])
    return x


def setup_inputs(seed: int = 0) -> dict:
    key = jax.random.key(seed)
    ks = jax.random.split(key, 24)
    f32 = jnp.float32

    def w(k, shape, fan_in):
        return jax.random.normal(k, shape, f32) * fan_in ** -0.5

    def gain(k, shape):
        return 1.0 + 0.02 * jax.random.normal(k, shape, f32)

    a_cols = (A_Q_HEADS + 2 * A_KV_HEADS) * HEAD_DIM
    b_cols = (B_Q_HEADS + 2 * B_KV_HEADS) * HEAD_DIM
    return {
        "x_prompt": jax.random.normal(ks[0], (BATCH, SEQ, D_MODEL), f32),
        "x_sample": jax.random.normal(ks[1], (DEC_BATCH, DEC_SEQ, D_MODEL), f32),
        "p_prompt": jax.random.normal(ks[2], (DEPTH, BATCH, SEQ, PLE_DIM), f32),
        "p_sample": jax.random.normal(ks[3], (DEPTH, DEC_BATCH, DEC_SEQ, PLE_DIM), f32),
        "norm_mix": gain(ks[4], (DEPTH, D_MODEL)),
        "norm_ffn": gain(ks[5], (DEPTH, D_MODEL)),
        "norm_ple": gain(ks[6], (DEPTH, D_MODEL)),
        "a_wqkv": w(ks[7], (N_A_LAYERS, D_MODEL, a_cols), D_MODEL),
        "a_wo": w(ks[8], (N_A_LAYERS, A_Q_HEADS * HEAD_DIM, D_MODEL), A_Q_HEADS * HEAD_DIM),
        "a_q_gain": gain(ks[9], (N_A_LAYERS, HEAD_DIM)),
        "a_k_gain": gain(ks[10], (N_A_LAYERS, HEAD_DIM)),
        "a_sink": 0.5 * jax.random.normal(ks[11], (N_A_LAYERS, A_Q_HEADS), f32),
        "b_wqkv": w(ks[12], (N_B_LAYERS, D_MODEL, b_cols), D_MODEL),
        "b_wo": w(ks[13], (N_B_LAYERS, B_Q_HEADS * HEAD_DIM, D_MODEL), B_Q_HEADS * HEAD_DIM),
        "b_q_gain": gain(ks[14], (N_B_LAYERS, HEAD_DIM)),
        "b_k_gain": gain(ks[15], (N_B_LAYERS, HEAD_DIM)),
        "ffn_w_gate": w(ks[16], (DEPTH, D_MODEL, FFN_HIDDEN), D_MODEL),
        "ffn_w_up": w(ks[17], (DEPTH, D_MODEL, FFN_HIDDEN), D_MODEL),
        "ffn_w_down": w(ks[18], (DEPTH, FFN_HIDDEN, D_MODEL), FFN_HIDDEN),
        "ple_w_gate": w(ks[19], (DEPTH, D_MODEL, D_MODEL), D_MODEL),
        "ple_w_proj": w(ks[20], (DEPTH, PLE_DIM, D_MODEL), PLE_DIM),
    }


def reference(x_prompt, x_sample, p_prompt, p_sample, norm_mix, norm_ffn, norm_ple,
              a_wqkv, a_wo, a_q_gain, a_k_gain, a_sink,
              b_wqkv, b_wo, b_q_gain, b_k_gain,
              ffn_w_gate, ffn_w_up, ffn_w_down, ple_w_gate, ple_w_proj):
    y_prompt = encoder_trunk(x_prompt, p_prompt, norm_mix, norm_ffn, norm_ple,
                             a_wqkv, a_wo, a_q_gain, a_k_gain, a_sink,
                             b_wqkv, b_wo, b_q_gain, b_k_gain,
                             ffn_w_gate, ffn_w_up, ffn_w_down, ple_w_gate, ple_w_proj)
    y_sample = encoder_trunk(x_sample, p_sample, norm_mix, norm_ffn, norm_ple,
                             a_wqkv, a_wo, a_q_gain, a_k_gain, a_sink,
                             b_wqkv, b_wo, b_q_gain, b_k_gain,
                             ffn_w_gate, ffn_w_up, ffn_w_down, ple_w_gate, ple_w_proj)
    return (y_prompt, y_sample)
```

```python
import contextlib
import numpy as np
import concourse.bass as bass
import concourse.mybir as mybir
from concourse.bass_utils import run_bass_kernel_spmd

F32 = mybir.dt.float32
BF16 = mybir.dt.bfloat16
I32 = mybir.dt.int32
AF = mybir.ActivationFunctionType
ALU = mybir.AluOpType

D = 1024
FF = 2816
NJ = FF // 128
PLE = 256
PAD = 1024
EPS = 1e-6
ST = 2048
TT = 512
WCAP = 5632
N_CORES = 8
DBG_SKIP = set()


class Buf:
    __slots__ = ("name", "w", "r")

    def __init__(self, name=""):
        self.name = name
        self.w = None
        self.r = []


def bufs(n, name=""):
    return [Buf(name + str(i)) for i in range(n)]


class Op:
    __slots__ = ("eng", "fn", "idx", "is_dma", "semkey", "dmaval", "waits", "target", "cnt", "gidx")


class Sched:
    ENGS = ("pe", "act", "dve", "pool", "sp")

    def __init__(self, nc, st):
        self.nc = nc
        self.ops = {e: [] for e in self.ENGS}
        self.nops = {e: 0 for e in self.ENGS}
        self.cnt = {e: 0 for e in self.ENGS}
        self.dma_cnt = {}
        self.dsem = {}
        self.esem = {e: st.enter_context(nc.semaphore("s_" + e)) for e in self.ENGS}
        self.st = st
        self.seen_eng = {e: {} for e in self.ENGS}
        self.seen_dma = {e: {} for e in self.ENGS}
        self.last_target = {e: None for e in self.ENGS}
        self.total = 0

    def op(self, eng, fn, reads=(), writes=(), dma=None):
        o = Op()
        o.eng = eng
        o.fn = fn
        o.gidx = self.nops[eng]
        self.nops[eng] += 1
        o.is_dma = dma is not None
        o.semkey = dma
        o.target = False
        o.cnt = None
        waits = []
        if o.is_dma:
            if dma not in self.dsem:
                self.dsem[dma] = self.st.enter_context(self.nc.semaphore("d%d" % len(self.dsem)))
                self.dma_cnt[dma] = 0
            self.dma_cnt[dma] += 16
            o.dmaval = self.dma_cnt[dma]
        for b in reads:
            d = b.w
            if d is not None:
                self._dep(o, d, "raw", waits)
        for b in writes:
            d = b.w
            if d is not None:
                self._dep(o, d, "waw", waits)
            for r in b.r:
                self._dep(o, r, "war", waits)
        o.waits = waits
        for b in reads:
            b.r.append(o)
        for b in writes:
            b.w = o
            b.r = []
        self.ops[eng].append(o)
        self.total += 1
        return o

    def _dep(self, o, d, kind, waits):
        if d is o:
            return
        if d.is_dma:
            if o.is_dma and d.semkey == o.semkey and kind == "waw":
                return
            waits.append(d)
        else:
            if d.eng == o.eng and not o.is_dma:
                if o.eng == "pe":
                    return
            waits.append(d)

    def flush(self, final=False):
        nc = self.nc
        plan = {}
        for e in self.ENGS:
            seen_eng = self.seen_eng[e]
            seen_dma = self.seen_dma[e]
            for o in self.ops[e]:
                need_eng = {}
                need_dma = {}
                for d in o.waits:
                    if d.is_dma:
                        if d.dmaval > seen_dma.get(d.semkey, 0):
                            need_dma[d.semkey] = max(need_dma.get(d.semkey, 0), d.dmaval)
                    else:
                        if d.gidx > seen_eng.get(d.eng, -1):
                            if d.eng not in need_eng or need_eng[d.eng].gidx < d.gidx:
                                need_eng[d.eng] = d
                for k, v in need_dma.items():
                    seen_dma[k] = v
                for k, d in need_eng.items():
                    seen_eng[k] = d.gidx
                    d.target = True
                o.waits = (list(need_eng.values()), list(need_dma.items()))
        barrier = {}
        for e in self.ENGS:
            last = None
            for o in self.ops[e]:
                if not o.is_dma:
                    last = o
            if last is not None:
                last.target = True
            c = self.cnt[e]
            for o in self.ops[e]:
                if o.target and not o.is_dma:
                    c += 1
                    o.cnt = c
            self.cnt[e] = c
        end_cnt = dict(self.cnt)
        end_dma = dict(self.dma_cnt)
        prev = getattr(self, "_prev_end", None)
        esem, dsem = self.esem, self.dsem
        ops = self.ops

        def run(engname, eng):
            if prev is not None:
                pc, pd = prev
                for e2, v in pc.items():
                    if e2 != engname and v > 0:
                        eng.wait_ge(esem[e2], v)
                for k, v in pd.items():
                    eng.wait_ge(dsem[k], v)
            for o in ops[engname]:
                we, wd = o.waits
                for d in we:
                    eng.wait_ge(esem[d.eng], d.cnt)
                for k, v in wd:
                    eng.wait_ge(dsem[k], v)
                ins = o.fn(eng)
                if o.is_dma:
                    ins.then_inc(dsem[o.semkey], 16)
                elif o.target:
                    ins.then_inc(esem[engname], 1)
            if final:
                for k, v in end_dma.items():
                    eng.wait_ge(dsem[k], v)

        with nc.Block() as block:
            @block.tensor
            def _(e):
                run("pe", e)

            @block.scalar
            def _(e):
                run("act", e)

            @block.vector
            def _(e):
                run("dve", e)

            @block.gpsimd
            def _(e):
                run("pool", e)

            @block.sync
            def _(e):
                run("sp", e)

        self._prev_end = (end_cnt, end_dma)
        for e in self.ENGS:
            self.seen_eng[e] = {e2: self.nops[e2] - 1 for e2 in self.ENGS}
            self.seen_dma[e] = dict(end_dma)
        self.ops = {e: [] for e in self.ENGS}


class Ring:
    def __init__(self, items):
        self.items = list(items)
        self.i = 0

    def next(self):
        it = self.items[self.i % len(self.items)]
        self.i += 1
        return it


def layer_cfg(l):
    c = {}
    if l == 0:
        c["hq"], c["hkv"] = 16, 4
        c["pairs"] = [(8 * jc + i, 8 * jc + 4 + i) for jc in range(2) for i in range(4)]
        c["kch"] = [m // 4 for m in range(8)]
        c["sets"] = [[m] for m in range(8)]
        c["dil"] = [1] * 8
        c["typeA"] = True
        c["radius"] = 128
        c["wo_groups"] = [(8 * jc, 8 * jc + 4, 4) for jc in range(2)]
    else:
        c["hq"], c["hkv"] = 18, 6
        c["pairs"] = [(6 * g + i, 6 * g + 3 + i) for g in range(3) for i in range(3)]
        c["kch"] = [m // 3 for m in range(9)]
        c["sets"] = [[i, 3 + i, 6 + i] for i in range(3)]
        c["dil"] = [4 ** (m // 3) for m in range(9)]
        c["typeA"] = False
        c["radius"] = 64
        c["wo_groups"] = [(6 * g, 6 * g + 3, 3) for g in range(3)]
    c["nq"] = len(c["pairs"])
    c["nkc"] = c["hkv"] // 2
    c["slopes"] = [2.0 ** (-8.0 * (h + 1) / c["hq"]) for h in range(c["hq"])]
    return c


def wblocks(l, cfg):
    blks = [("wo", 0), ("wo", 1)]
    blks += [("gu", jj) for jj in range(NJ // 2)]
    blks += [("dn", mp) for mp in range(4)]
    blks += [("pg", 0), ("pg", 1), ("pp", 0)]
    return blks


def build(seq_lens, debug=False, stop_after=None):
    nc = bass.Bass("TRN2", target_bir_lowering=False)
    NT = sum(seq_lens)
    nseq = len(seq_lens)
    NTP = NT + PAD * (nseq + 1)
    offs = [sum(seq_lens[:i]) for i in range(nseq)]
    poffs = [offs[i] + PAD * (i + 1) for i in range(nseq)]

    def dt_in(name, shape):
        return nc.dram_tensor(name, shape, F32, kind="ExternalInput").ap()

    def dt_scr(name, shape, dt):
        return nc.dram_tensor(name, shape, dt, kind=("ExternalOutput" if debug else "Internal")).ap()

    x_d = dt_in("x", [NT, D])
    p_d = dt_in("p", [2, NT, PLE])
    norm_mix = dt_in("norm_mix", [2, D])
    norm_ffn = dt_in("norm_ffn", [2, D])
    norm_ple = dt_in("norm_ple", [2, D])
    a_wqkv = dt_in("a_wqkv", [D, 1536])
    a_wo = dt_in("a_wo", [1024, D])
    a_qg = dt_in("a_q_gain", [64])
    a_kg = dt_in("a_k_gain", [64])
    a_sink = dt_in("a_sink", [16])
    b_wqkv = dt_in("b_wqkv", [D, 1920])
    b_wo = dt_in("b_wo", [1152, D])
    b_qg = dt_in("b_q_gain", [64])
    b_kg = dt_in("b_k_gain", [64])
    w_gate = dt_in("ffn_w_gate", [2, D, FF])
    w_up = dt_in("ffn_w_up", [2, D, FF])
    w_down = dt_in("ffn_w_down", [2, FF, D])
    w_pg = dt_in("ple_w_gate", [2, D, D])
    w_pp = dt_in("ple_w_proj", [2, PLE, D])
    y_d = nc.dram_tensor("y", [NT, D], F32, kind="ExternalOutput").ap()

    wqkv_d = [a_wqkv, b_wqkv]
    wo_d = [a_wo, b_wo]
    qg_d = [a_qg, b_qg]
    kg_d = [a_kg, b_kg]

    NBLK = 20
    wsc = dt_scr("wsc", [2 * NBLK, 128, WCAP], BF16)
    xT0 = dt_scr("xT0", [8, 128, NT], F32)
    x1T = dt_scr("x1T", [8, 128, NT], F32)
    qT_d = dt_scr("qT", [9, 128, NT], BF16)
    kT_d = dt_scr("kT", [3, 128, NTP], BF16)
    vx_d = dt_scr("vx", [NTP, 6, 128], BF16)
    aT_d = dt_scr("aT", [9, 128, NT], BF16)

    ntile = NT // TT
    B_xT0 = bufs(ntile, "xT0_")
    B_x1T = bufs(ntile, "x1T_")
    B_q = bufs(ntile, "q_")
    B_k = bufs(ntile, "k_")
    B_v = bufs(ntile, "v_")
    B_a = bufs(ntile, "a_")
    B_y = bufs(ntile, "y_")
    B_wsc = bufs(2 * NBLK, "wsc_")
    B_pad = Buf("pad")

    def tile_pt0(t):
        t0 = t * TT
        for s in range(nseq):
            if offs[s] <= t0 < offs[s] + seq_lens[s]:
                return poffs[s] + (t0 - offs[s])
        raise AssertionError

    with contextlib.ExitStack() as gst:
        S = Sched(nc, gst)

        uniq = [0]

        def sb(st, name, shape, dt):
            uniq[0] += 1
            return st.enter_context(nc.sbuf_tensor("%s_%d" % (name, uniq[0]), shape, dt))

        def psum(st, name):
            return st.enter_context(nc.psum_tensor(name, [128, 512], F32))

        ident = sb(gst, "ident", [128, 128], F32)
        ones_bf = sb(gst, "ones_bf", [128, 128], BF16)
        blk_bf = sb(gst, "blk_bf", [128, 128], BF16)
        gmix = sb(gst, "gmix", [128, 2, 8], F32)
        gffn = sb(gst, "gffn", [128, 2, 8], F32)
        gple = sb(gst, "gple", [128, 2, 8], F32)
        gq = sb(gst, "gq", [128, 2], F32)
        gk = sb(gst, "gk", [128, 2], F32)
        esink = sb(gst, "esink", [128, 16], F32)
        B_const = Buf("const")

        PSP = [gst.enter_context(nc.psum_tensor("psp%d" % i, [128, 1024], F32)) for i in range(4)]
        PS = [PSP[i // 2][:, (i % 2) * 512:(i % 2 + 1) * 512] for i in range(8)]
        B_PS = bufs(8, "ps")

        S.op("pool", lambda e: e.memset(ident[:], 0.0), writes=[B_const])
        S.op("pool", lambda e: e.affine_select(out=ident[:], in_=ident[:], pattern=[[-1, 128]],
                                               compare_op=ALU.not_equal, fill=1.0, base=0,
                                               channel_multiplier=1), reads=[B_const], writes=[B_const])
        S.op("pool", lambda e: e.memset(ones_bf[:], 1.0), writes=[B_const])
        S.op("pool", lambda e: e.memset(blk_bf[:], 0.0), writes=[B_const])
        S.op("pool", lambda e: e.memset(blk_bf[0:64, 0:64], 1.0), writes=[B_const])
        S.op("pool", lambda e: e.memset(blk_bf[64:128, 64:128], 1.0), writes=[B_const])
        for l in range(2):
            for (g_sb, g_d) in ((gmix, norm_mix), (gffn, norm_ffn), (gple, norm_ple)):
                S.op("sp", lambda e, g_sb=g_sb, g_d=g_d, l=l: e.dma_start(
                    out=g_sb[:, l, :], in_=g_d[l].rearrange("(c p) -> p c", p=128),
                    allow_slow_non_contiguous=True), writes=[B_const], dma="const")
            for half in range(2):
                S.op("sp", lambda e, l=l, half=half: e.dma_start(
                    out=gq[half * 64:(half + 1) * 64, l:l + 1], in_=qg_d[l].rearrange("(p o) -> p o", o=1),
                    allow_slow_non_contiguous=True), writes=[B_const], dma="const")
                S.op("sp", lambda e, l=l, half=half: e.dma_start(
                    out=gk[half * 64:(half + 1) * 64, l:l + 1], in_=kg_d[l].rearrange("(p o) -> p o", o=1),
                    allow_slow_non_contiguous=True), writes=[B_const], dma="const")
        S.op("sp", lambda e: e.dma_start(out=esink[:], in_=a_sink.partition_broadcast(128),
                                         allow_slow_non_contiguous=True), writes=[B_const], dma="const")
        S.op("act", lambda e: e.activation(out=esink[:], in_=esink[:], func=AF.Exp), reads=[B_const], writes=[B_const])

        with contextlib.ExitStack() as st:
            zt = sb(st, "zt", [128, 2048], BF16)
            B_zt = Buf("zt")
            S.op("pool", lambda e: e.memset(zt[:], 0.0), writes=[B_zt])
            pad_starts = [0] + [poffs[i] + seq_lens[i] for i in range(nseq)]
            for ps0 in pad_starts:
                for kc in range(3):
                    S.op("pool", lambda e, ps0=ps0, kc=kc: e.dma_start(out=kT_d[kc, :, ps0:ps0 + PAD], in_=zt[:, 0:PAD]),
                         reads=[B_zt], writes=[B_pad], dma="zt")
                for q4 in range(4):
                    S.op("pool", lambda e, ps0=ps0, q4=q4: e.dma_start(
                        out=vx_d[ps0 + q4 * 256:ps0 + (q4 + 1) * 256].rearrange("(p a) h c -> p (a h c)", p=128),
                        in_=zt[:, 0:1536]), reads=[B_zt], writes=[B_pad], dma="zt")
            S.flush()

        HCAP = 2816

        def make_conv_steps(l, st):
            wst = [sb(st, "cwst%d" % i, [128, HCAP], F32) for i in range(2)]
            wbf = [sb(st, "cwbf%d" % i, [128, HCAP], BF16) for i in range(2)]
            B_wst = bufs(2, "cwst")
            B_wbf = bufs(2, "cwbf")
            steps = []
            k = 0
            cfg = layer_cfg(l)
            for bi, (kind, idx) in enumerate(wblocks(l, cfg)):
                nch, ncol = {"wo": (cfg["nq"], 512), "gu": (16, 256), "dn": (NJ, 256), "pg": (8, 512), "pp": (2, 1024)}[kind]
                hc = (nch + 1) // 2
                for (c0, c1) in ((0, hc), (hc, nch)):
                    slot = k % 2
                    k += 1

                    def step(bi=bi, kind=kind, idx=idx, slot=slot, c0=c0, c1=c1, ncol=ncol):
                        ne = (c1 - c0) * ncol
                        dstv = wst[slot][:, 0:ne].rearrange("p (c n) -> p c n", n=ncol)
                        dmas = []
                        if kind == "wo":
                            wo_h = wo_d[l].rearrange("(h d) n -> d h n", d=64)
                            m0 = 0
                            for (a0, b0, n) in cfg["wo_groups"]:
                                lo, hi = max(c0, m0), min(c1, m0 + n)
                                if lo < hi:
                                    dmas.append((dstv[0:64, lo - c0:hi - c0, :], wo_h[:, a0 + lo - m0:a0 + hi - m0, idx * 512:(idx + 1) * 512]))
                                    dmas.append((dstv[64:128, lo - c0:hi - c0, :], wo_h[:, b0 + lo - m0:b0 + hi - m0, idx * 512:(idx + 1) * 512]))
                                m0 += n
                        elif kind == "gu":
                            src = (w_gate if c0 == 0 else w_up)[l].rearrange("(c p) n -> p c n", p=128)
                            dmas.append((dstv, src[:, :, idx * 256:(idx + 1) * 256]))
                        elif kind == "dn":
                            dmas.append((dstv, w_down[l].rearrange("(c p) n -> p c n", p=128)[:, c0:c1, idx * 256:(idx + 1) * 256]))
                        elif kind == "pg":
                            dmas.append((dstv, w_pg[l].rearrange("(c p) n -> p c n", p=128)[:, c0:c1, idx * 512:(idx + 1) * 512]))
                        else:
                            dmas.append((dstv, w_pp[l].rearrange("(c p) n -> p c n", p=128)[:, c0:c1, :]))
                        for (dst, src) in dmas:
                            S.op("sp", lambda e, dst=dst, src=src: e.dma_start(out=dst, in_=src),
                                 writes=[B_wst[slot]], dma="cwst%d" % slot)
                        S.op("pool", lambda e: e.tensor_copy(out=wbf[slot][:, 0:ne], in_=wst[slot][:, 0:ne]),
                             reads=[B_wst[slot]], writes=[B_wbf[slot]])
                        gb = l * NBLK + bi
                        S.op("pool", lambda e: e.dma_start(out=wsc[gb, :, c0 * ncol:c1 * ncol], in_=wbf[slot][:, 0:ne]),
                             reads=[B_wbf[slot]], writes=[B_wsc[gb]], dma="cwbf%d" % slot)
                    steps.append(step)
            return steps

        def norm_sq(xs, B_xs, sq, B_sqc, c, eng="act"):
            if eng == "act":
                S.op("act", lambda e: e.activation(out=sq[:, c, :], in_=xs[:, c, :], func=AF.Square), reads=[B_xs[c]], writes=[B_sqc[c]])
            else:
                S.op(eng, lambda e: e.tensor_tensor(out=sq[:, c, :], in0=xs[:, c, :], in1=xs[:, c, :], op=ALU.mult), reads=[B_xs[c]], writes=[B_sqc[c]])

        def norm_mm(sq, B_sqc, ps_i, c):
            S.op("pe", lambda e: e.matmul(PS[ps_i][:, :], lhsT=ones_bf[:], rhs=sq[:, c, :], start=(c == 0), stop=(c == 7)),
                 reads=[B_sqc[c], B_const], writes=[B_PS[ps_i]])

        def norm_rstd(lnv, B_lnv, ps_i):
            S.op("act", lambda e: e.activation(out=lnv[:], in_=PS[ps_i][:, :], func=AF.Ln, scale=1.0 / D, bias=EPS),
                 reads=[B_PS[ps_i]], writes=[B_lnv])
            S.op("act", lambda e: e.activation(out=lnv[:], in_=lnv[:], func=AF.Exp, scale=-0.5), reads=[B_lnv], writes=[B_lnv])

        def norm_apply(xs, B_xs, gcol, hn, B_hn, lnv, B_lnv, c):
            S.op("dve", lambda e: e.scalar_tensor_tensor(out=hn[:, c, :], in0=xs[:, c, :], scalar=gcol[:, c:c + 1],
                                                          in1=lnv[:], op0=ALU.mult, op1=ALU.mult),
                 reads=[B_xs[c], B_lnv, B_const], writes=[B_hn[c]])

        def norm_fin(xs, B_xs, gcol, hn, B_hn, lnv, B_lnv, ps_i):
            norm_rstd(lnv, B_lnv, ps_i)
            for c in range(8):
                norm_apply(xs, B_xs, gcol, hn, B_hn, lnv, B_lnv, c)

        def rmsnorm(xs, B_xs, gcol, hn, B_hn, sq, B_sq, lnv, B_lnv, ps_i):
            S.op("act", lambda e: e.activation(out=sq[:], in_=xs[:], func=AF.Square), reads=B_xs, writes=[B_sq])
            for c in range(8):
                S.op("pe", lambda e, c=c: e.matmul(PS[ps_i][:, :], lhsT=ones_bf[:], rhs=sq[:, c, :], start=(c == 0), stop=(c == 7)),
                     reads=[B_sq, B_const], writes=[B_PS[ps_i]])
            S.op("act", lambda e: e.activation(out=lnv[:], in_=PS[ps_i][:, :], func=AF.Ln, scale=1.0 / D, bias=EPS),
                 reads=[B_PS[ps_i]], writes=[B_lnv])
            S.op("act", lambda e: e.activation(out=lnv[:], in_=lnv[:], func=AF.Exp, scale=-0.5), reads=[B_lnv], writes=[B_lnv])
            for c in range(8):
                S.op("dve", lambda e, c=c: e.scalar_tensor_tensor(out=hn[:, c, :], in0=xs[:, c, :], scalar=gcol[:, c:c + 1],
                                                                  in1=lnv[:], op0=ALU.mult, op1=ALU.mult),
                     reads=[B_xs[c], B_lnv, B_const], writes=[B_hn[c]])

        for l in range(2):
            cfg = layer_cfg(l)
            nq, nkc, hkv, hq = cfg["nq"], cfg["nkc"], cfg["hkv"], cfg["hq"]
            nqk = nq + nkc
            B_xin = B_xT0 if l == 0 else B_x1T
            xin_d = xT0 if l == 0 else x1T
            B_xout = B_x1T if l == 0 else None

            with contextlib.ExitStack() as st:
                wqk = sb(st, "wqk", [128, 8, nqk * 128], BF16)
                wv = sb(st, "wv", [128, 8, hkv * 64], BF16)
                wtmp = [sb(st, "wtmp%d" % i, [128, 8, 128], F32) for i in range(2)]
                B_wtmp = bufs(2, "wtmp")
                B_wqk = Buf("wqk")
                xtok = [sb(st, "xtok%d" % i, [128, 4, D], F32) for i in range(2)] if l == 0 else None
                B_xtok = bufs(2, "xtok")
                xs2 = [sb(st, "xs%d" % i, [128, 8, TT], F32) for i in range(2)]
                B_xs2 = [bufs(8, "xs%d_" % i) for i in range(2)]
                hn2 = [sb(st, "hn%d" % i, [128, 8, TT], BF16) for i in range(2)]
                B_hn2 = [bufs(8, "hn%d_" % i) for i in range(2)]
                sq = sb(st, "sq", [128, 8, TT], BF16)
                B_sq = Buf("sq")
                lnv = sb(st, "lnv", [128, TT], F32)
                B_lnv = Buf("lnv")
                qko2 = [sb(st, "qko%d" % i, [128, nqk, TT], BF16) for i in range(2)]
                B_qko2 = [bufs(nqk, "qko%d_" % i) for i in range(2)]
                vxo2 = [sb(st, "vxo%d" % i, [128, 4, hkv, 128], BF16) for i in range(2)]
                B_vxo2 = [bufs(4, "vxo%d_" % i) for i in range(2)]
                hsq = [sb(st, "hsq%d" % i, [128, TT], BF16) for i in range(2)]
                B_hsq = bufs(2, "hsq")
                hln = [sb(st, "hln%d" % i, [128, TT], F32) for i in range(2)]
                B_hln = bufs(2, "hln")

                wsrc = wqkv_d[l].rearrange("(c p) n -> p c n", p=128)
                pieces = []
                for m, (a, b) in enumerate(cfg["pairs"]):
                    pieces.append((wqk[:, :, m * 128:m * 128 + 64], a * 64, 64))
                    pieces.append((wqk[:, :, m * 128 + 64:m * 128 + 128], b * 64, 64))
                for jc in range(nkc):
                    pieces.append((wqk[:, :, (nq + jc) * 128:(nq + jc + 1) * 128], hq * 64 + jc * 128, 128))
                for h0 in range(0, hkv * 64, 128):
                    pieces.append((wv[:, :, h0:h0 + 128], (hq + hkv) * 64 + h0, 128))
                for i, (dst, c0, n) in enumerate(pieces):
                    slot = i % 2
                    S.op("sp", lambda e, slot=slot, c0=c0, n=n: e.dma_start(out=wtmp[slot][:, :, 0:n], in_=wsrc[:, :, c0:c0 + n]),
                         writes=[B_wtmp[slot]], dma="wtmp%d" % slot)
                    eng = "dve" if i % 2 == 0 else "pool"
                    S.op(eng, lambda e, dst=dst, slot=slot, n=n: e.tensor_copy(out=dst, in_=wtmp[slot][:, :, 0:n]),
                         reads=[B_wtmp[slot]], writes=[B_wqk])
                for i in range(2):
                    S.op("pool", lambda e, i=i: e.memset(vxo2[i][:], 1.0), writes=B_vxo2[i])

                def p1_loads(t):
                    slot = t % 2
                    t0 = t * TT
                    if l == 0:
                        S.op("sp", lambda e: e.dma_start(out=xtok[slot][:], in_=x_d[t0:t0 + TT, :].rearrange("(c p) d -> p c d", p=128)),
                             writes=[B_xtok[slot]], dma="xtok%d" % slot)
                    else:
                        S.op("sp", lambda e: e.dma_start(out=xs2[slot][:], in_=xin_d[:, :, t0:t0 + TT].rearrange("c p t -> p c t")),
                             reads=[B_xin[t]], writes=B_xs2[slot], dma="xs%d" % slot)

                ps_tr = Ring([0, 1])
                ps_q = Ring([3, 4, 5])
                ps_h = Ring([6, 7])

                B_sqc = bufs(8, "sqc")

                def p1_frontA(t):
                    slot = t % 2
                    t0 = t * TT
                    xs, Bx = xs2[slot], B_xs2[slot]
                    if l == 0:
                        for c in range(8):
                            pi = ps_tr.next()
                            for tc in range(4):
                                S.op("pe", lambda e, c=c, tc=tc, pi=pi: e.transpose(
                                    out=PS[pi][:, tc * 128:(tc + 1) * 128], in_=xtok[slot][:, tc, c * 128:(c + 1) * 128], identity=ident[:]),
                                    reads=[B_xtok[slot], B_const], writes=[B_PS[pi]])
                            if c % 2 == 0:
                                S.op("act", lambda e, c=c, pi=pi: e.copy(out=xs[:, c, :], in_=PS[pi][:, :]), reads=[B_PS[pi]], writes=[Bx[c]])
                            else:
                                S.op("dve", lambda e, c=c, pi=pi: e.tensor_copy(out=xs[:, c, :], in_=PS[pi][:, :]), reads=[B_PS[pi]], writes=[Bx[c]])
                        S.op("pool", lambda e: e.dma_start(out=xT0[:, :, t0:t0 + TT].rearrange("c p t -> p c t"), in_=xs[:]),
                             reads=Bx, writes=[B_xT0[t]], dma="xs_st%d" % slot)
                    for c in range(8):
                        norm_sq(xs, Bx, sq, B_sqc, c, eng=("act" if l == 0 else "dve"))

                def p1_frontB(t):
                    for c in range(8):
                        norm_mm(sq, B_sqc, 2, c)
                    norm_rstd(lnv, B_lnv, 2)

                def p1_frontC(t, cs):
                    slot = t % 2
                    xs, Bx = xs2[slot], B_xs2[slot]
                    hn, Bh = hn2[slot], B_hn2[slot]
                    for c in cs:
                        norm_apply(xs, Bx, gmix[:, l, :], hn, Bh, lnv, B_lnv, c)

                def p1_compute(t, hooks):
                    slot = t % 2
                    t0 = t * TT
                    pt0 = tile_pt0(t)
                    xs, Bx = xs2[slot], B_xs2[slot]
                    hn, Bh = hn2[slot], B_hn2[slot]
                    qko, Bq = qko2[slot], B_qko2[slot]
                    vxo, Bv = vxo2[slot], B_vxo2[slot]
                    pend = []
                    for m in range(nqk):
                        if m in hooks:
                            hooks[m]()
                        pi = ps_q.next()
                        for kc in range(8):
                            S.op("pe", lambda e, m=m, kc=kc, pi=pi: e.matmul(PS[pi][:, :], lhsT=wqk[:, kc, m * 128:(m + 1) * 128], rhs=hn[:, kc, :],
                                                                             start=(kc == 0), stop=(kc == 7)),
                                 reads=[B_wqk, Bh[kc]], writes=[B_PS[pi]])
                        hs = m % 2
                        qi3 = m % 3
                        S.op("act", lambda e, pi=pi, hs=hs: e.activation(out=hsq[hs][:], in_=PS[pi][:, :], func=AF.Square),
                             reads=[B_PS[pi]], writes=[B_hsq[hs]])

                        def tail(m=m, pi=pi, hs=hs, qi3=qi3):
                            ph = ps_h.next()
                            S.op("pe", lambda e: e.matmul(PS[ph][:, :], lhsT=blk_bf[:], rhs=hsq[hs][:], start=True, stop=True),
                                 reads=[B_hsq[hs], B_const], writes=[B_PS[ph]])
                            S.op("act", lambda e: e.activation(out=hln[hs][:], in_=PS[ph][:, :], func=AF.Ln, scale=1.0 / 64, bias=EPS),
                                 reads=[B_PS[ph]], writes=[B_hln[hs]])
                            S.op("act", lambda e: e.activation(out=hln[hs][:], in_=hln[hs][:], func=AF.Exp, scale=-0.5),
                                 reads=[B_hln[hs]], writes=[B_hln[hs]])
                            gcol = gq[:, l:l + 1] if m < nq else gk[:, l:l + 1]
                            S.op("dve", lambda e: e.scalar_tensor_tensor(
                                out=qko[:, m, :], in0=PS[pi][:, :], scalar=gcol, in1=hln[hs][:], op0=ALU.mult, op1=ALU.mult),
                                reads=[B_PS[pi], B_hln[hs], B_const], writes=[Bq[m]])
                        pend.append(tail)
                        if len(pend) > 1:
                            pend.pop(0)()
                    while pend:
                        pend.pop(0)()
                    nv = hkv * 64
                    for tc in range(4):
                        pi = ps_tr.next()
                        for kc in range(8):
                            S.op("pe", lambda e, tc=tc, kc=kc, pi=pi: e.matmul(PS[pi][:, 0:nv], lhsT=hn[:, kc, tc * 128:(tc + 1) * 128], rhs=wv[:, kc, :],
                                                                               start=(kc == 0), stop=(kc == 7)),
                                 reads=[B_wqk, Bh[kc]], writes=[B_PS[pi]])
                        pv = PS[pi][:, 0:nv].rearrange("p (h two d) -> p h two d", two=2, d=64)
                        vv = vxo[:, tc, :, :].rearrange("p (h two) c -> p h two c", two=2)
                        S.op("act", lambda e, pv=pv, vv=vv: e.copy(out=vv[:, :, 0, 0:64], in_=pv[:, :, 0, :]), reads=[B_PS[pi]], writes=[Bv[tc]])
                        S.op("dve", lambda e, pv=pv, vv=vv: e.tensor_copy(out=vv[:, :, 1, 64:128], in_=pv[:, :, 1, :]), reads=[B_PS[pi]], writes=[Bv[tc]])
                    S.op("pool", lambda e: e.dma_start(out=qT_d[0:nq, :, t0:t0 + TT].rearrange("c p t -> p c t"), in_=qko[:, 0:nq, :]),
                         reads=Bq[0:nq], writes=[B_q[t]], dma="q_st%d" % slot)
                    S.op("pool", lambda e: e.dma_start(out=kT_d[0:nkc, :, pt0:pt0 + TT].rearrange("c p t -> p c t"), in_=qko[:, nq:nqk, :]),
                         reads=Bq[nq:nqk], writes=[B_k[t]], dma="k_st%d" % slot)
                    S.op("pool", lambda e: e.dma_start(out=vx_d[pt0:pt0 + TT, 0:hkv, :].rearrange("(c p) h d -> p c h d", p=128), in_=vxo[:]),
                         reads=Bv, writes=[B_v[t]], dma="v_st%d" % slot)

                conv_steps = make_conv_steps(0, st) if l == 0 else []
                per_tile = -(-len(conv_steps) // ntile) if conv_steps else 0
                p1_loads(0)
                p1_frontA(0)
                p1_frontB(0)
                p1_frontC(0, range(8))
                for t in range(ntile):
                    for _ in range(per_tile):
                        if conv_steps:
                            conv_steps.pop(0)()
                    hooks = {}
                    if t + 1 < ntile:
                        p1_loads(t + 1)
                        hooks[1] = (lambda t=t: p1_frontA(t + 1))
                        hooks[4] = (lambda t=t: p1_frontB(t + 1))
                        hooks[6] = (lambda t=t: p1_frontC(t + 1, (0, 1)))
                        hooks[7] = (lambda t=t: p1_frontC(t + 1, (2, 3)))
                        hooks[8] = (lambda t=t: p1_frontC(t + 1, (4, 5)))
                        hooks[9] = (lambda t=t: p1_frontC(t + 1, (6, 7)))
                    p1_compute(t, hooks)
                while conv_steps:
                    conv_steps.pop(0)()
                S.flush(final=(stop_after == (l, 1)))
            if stop_after == (l, 1):
                return nc

            with contextlib.ExitStack() as st:
                typeA = cfg["typeA"]
                R = cfg["radius"]
                EW = 384 if typeA else 256
                eoff = 128 if typeA else 64
                Et = sb(st, "Et", [128, hq, EW], BF16)
                Ef = sb(st, "Ef", [128, EW], F32)
                B_E = Buf("E")
                dI = sb(st, "dI", [128, EW], I32)
                dF = sb(st, "dF", [128, EW], F32)
                B_dI = Buf("dI")
                S.op("pool", lambda e: e.iota(dI[:], pattern=[[1, EW]], base=-eoff, channel_multiplier=-1), writes=[B_dI])
                S.op("dve", lambda e: e.tensor_copy(out=dF[:], in_=dI[:]), reads=[B_dI], writes=[B_dI])
                S.op("act", lambda e: e.activation(out=dF[:], in_=dF[:], func=AF.Abs), reads=[B_dI], writes=[B_dI])
                for h in range(hq):
                    dil_h = 1 if typeA else 4 ** (h // 6)
                    sc = -cfg["slopes"][h] * dil_h
                    S.op("act", lambda e, h=h, sc=sc: e.activation(out=Ef[:], in_=dF[:], func=AF.Exp, scale=sc), reads=[B_dI, B_E], writes=[B_E])
                    S.op("pool", lambda e, h=h: e.affine_select(out=Ef[:], in_=Ef[:], pattern=[[-1, EW]], compare_op=ALU.is_ge,
                                                                fill=0.0, base=R + eoff, channel_multiplier=1), reads=[B_E], writes=[B_E])
                    S.op("pool", lambda e, h=h: e.affine_select(out=Ef[:], in_=Ef[:], pattern=[[1, EW]], compare_op=ALU.is_ge,
                                                                fill=0.0, base=R - eoff, channel_multiplier=-1), reads=[B_E], writes=[B_E])
                    S.op("dve", lambda e, h=h: e.tensor_copy(out=Et[:, h, :], in_=Ef[:]), reads=[B_E], writes=[B_E])

                kT_sb = sb(st, "kT_sb", [128, nkc, 4096], BF16)
                B_kT = bufs(nkc, "kTsb")
                if typeA:
                    vx_sb = [sb(st, "vx_sb", [128, 18, 4, 128], BF16)]
                else:
                    vx_sb = [sb(st, "vx_sb%d" % g, [128, 4 ** g, 16 // (4 ** g) + 1, 2, 128], BF16) for g in range(3)]
                B_vx = bufs(len(vx_sb), "vxsb")
                nmem = len(cfg["sets"][0])
                qsb = [[sb(st, "qsb%d_%d" % (i, hf), [128, ST], BF16) for hf in range(2)] for i in range(3)]
                B_qsb = bufs(3, "qsb")
                for i in range(3):
                    S.op("pool", lambda e, i=i: e.memset(qsb[i][0][64:128, :], 0.0), writes=[B_qsb[i]])
                    S.op("pool", lambda e, i=i: e.memset(qsb[i][1][0:64, :], 0.0), writes=[B_qsb[i]])
                stage = [[sb(st, "stg%d_%d" % (i, hh), [128, ST], F32) for hh in range(2)] for i in range(nmem)] if not typeA else None
                B_stage = [[Buf("stg") for hh in range(2)] for i in range(nmem)]
                aout = [sb(st, "aout%d" % i, [128, ST], BF16) for i in range(2 * nmem)]
                B_aout = bufs(2 * nmem, "aout")
                dsum1 = sb(st, "dsum", [128, ST], F32) if not typeA else None
                dsum = [dsum1, dsum1]
                B_dsum = bufs(2, "dsum")
                pexp = [sb(st, "pexp%d" % i, [128, 2, 384], BF16) for i in range(4)]
                B_pexp = bufs(4, "pexp")
                NPT = 8
                LA = 5
                conv_steps = []
                Pt = [sb(st, "Pt%d" % i, [128, 2, 384], BF16) for i in range(NPT)]
                B_Pt = bufs(NPT, "Pt")
                pt_ring = Ring(list(range(NPT)))
                rd = [sb(st, "rd%d" % i, [128, 512], F32) for i in range(2)] if typeA else None
                B_rd = bufs(2, "rd")
                rd_ring = Ring([0, 1])
                ps_S = Ring([0, 1])
                ps_O = Ring([(4, 5), (6, 7)])
                mul_ring = Ring(["dve", "dve", "pool"])
                dsh = sb(st, "dsh", [128, ST], F32) if not typeA else None
                B_dsh = Buf("dsh")
                dq = []

                def defer(fn, is_item):
                    dq.append((fn, is_item))

                def tick():
                    while sum(1 for (_, it) in dq if it) > LA:
                        fn, it = dq.pop(0)
                        fn()

                def drain():
                    while dq:
                        fn, it = dq.pop(0)
                        fn()

                pe_ring = Ring([0, 1, 2, 3])
                q_ring = Ring([0, 1, 2])
                a_ring = Ring(list(range(2 * nmem)))

                for s in range(nseq):
                    for stile in range(seq_lens[s] // ST):
                        T0 = offs[s] + stile * ST
                        PT0 = poffs[s] + stile * ST
                        tl = list(range(T0 // TT, T0 // TT + 4))
                        lo_t = max(offs[s], T0 - PAD) // TT
                        hi_t = (min(offs[s] + seq_lens[s], T0 + ST + PAD) - 1) // TT
                        halo_k = [B_k[t] for t in range(lo_t, hi_t + 1)] + [B_pad]
                        halo_v = [B_v[t] for t in range(lo_t, hi_t + 1)] + [B_pad]
                        for kc in range(nkc):
                            S.op("sp", lambda e, kc=kc, PT0=PT0: e.dma_start(out=kT_sb[:, kc, :], in_=kT_d[kc, :, PT0 - PAD:PT0 + ST + PAD]),
                                 reads=halo_k, writes=[B_kT[kc]], dma="kTsb%d" % kc)
                        if typeA:
                            for b4 in range(3):
                                S.op("sp", lambda e, b4=b4, PT0=PT0: e.dma_start(
                                    out=vx_sb[0][:, b4 * 6:(b4 + 1) * 6, :, :],
                                    in_=vx_d[PT0 - 128 + b4 * 768:PT0 - 128 + (b4 + 1) * 768, 0:4, :].rearrange("(b p) h c -> p b h c", p=128)),
                                    reads=halo_v, writes=[B_vx[0]], dma="vxsb")
                        else:
                            for g in range(3):
                                dil = 4 ** g
                                nqb = 16 // dil
                                for r in range(dil):
                                    base = PT0 - 64 * dil
                                    nb = nqb + 1
                                    src = vx_d[base:base + nb * 128 * dil, 2 * g:2 * g + 2, :].rearrange(
                                        "(b p r) h c -> r p b h c", p=128, r=dil)[r]
                                    bsplit = [(0, nb)] if nb <= 8 else [(0, 6), (6, 12), (12, nb)]
                                    for (b0, b1) in bsplit:
                                        S.op("sp", lambda e, g=g, r=r, src=src, b0=b0, b1=b1: e.dma_start(
                                            out=vx_sb[g][:, r, b0:b1, :, :], in_=src[:, b0:b1, :, :]),
                                            reads=halo_v, writes=[B_vx[g]], dma="vxsb%d" % g)

                        for cset in cfg["sets"]:
                            outs = []
                            for mi, m in enumerate(cset):
                                (ha, hb) = cfg["pairs"][m]
                                kc = cfg["kch"][m]
                                dil = cfg["dil"][m]
                                g = 0 if typeA else m // 3
                                nqb = 16 // dil
                                qi = q_ring.next()
                                for hf in range(2):
                                    S.op("sp", lambda e, qi=qi, m=m, T0=T0, hf=hf: e.dma_start(
                                        out=qsb[qi][hf][hf * 64:hf * 64 + 64, :], in_=qT_d[m, hf * 64:hf * 64 + 64, T0:T0 + ST]),
                                        reads=[B_q[t] for t in tl], writes=[B_qsb[qi]], dma="qsb%d" % qi)
                                qv2 = [qsb[qi][hf][:].rearrange("p (i r) -> p r i", r=dil) for hf in range(2)]
                                if conv_steps:
                                    conv_steps.pop(0)()
                                kv = kT_sb[:, kc, :].rearrange("p (i r) -> p r i", r=dil)
                                kbase = PAD // dil
                                ai = a_ring.next()
                                outs.append(ai)
                                units = [(r, qb) for r in range(dil) for qb in range(nqb)]
                                for sg in range(len(units) // 4):
                                    us = units[sg * 4:sg * 4 + 4]
                                    (oa, ob) = ps_O.next()
                                    jobs = []
                                    cls = []
                                    for (r, qb) in us:
                                        if r not in cls:
                                            cls.append(r)
                                    for r in cls:
                                        qbs = [qb for (r2, qb) in us if r2 == r]
                                        col0 = [i for i, u in enumerate(us) if u[0] == r][0] * 128
                                        if typeA:
                                            brange = range(qbs[0] - 1, qbs[-1] + 2)
                                        else:
                                            brange = range(qbs[0], qbs[-1] + 2)
                                        for b in brange:
                                            qa = max(qbs[0], b - 1)
                                            qz = min(qbs[-1], b + 1 if typeA else b)
                                            n = (qz - qa + 1) * 128
                                            c0 = col0 + (qa - qbs[0]) * 128
                                            e0 = (qa - (b - 1)) * 128
                                            if typeA:
                                                kcol = kbase + 128 * b
                                                vap = vx_sb[0][:, b + 1, 2 * kc:2 * kc + 2, :]
                                            else:
                                                kcol = kbase + 128 * b - 64
                                                vap = vx_sb[g][:, r, b, :, :]
                                            jobs.append((r, qa, n, c0, e0, kcol, vap))
                                    for ji, (r, qa, n, c0, e0, kcol, vap) in enumerate(jobs):
                                        first = ji == 0
                                        last = ji == len(jobs) - 1
                                        sp_ = ps_S.next()
                                        for half in range(2):
                                            qv = qv2[half]
                                            S.op("pe", lambda e, sp_=sp_, half=half, r=r, kcol=kcol, qa=qa, n=n, kv=kv, qv=qv: e.matmul(
                                                PSP[sp_][:, half * 512:half * 512 + n], lhsT=kv[:, r, kcol:kcol + 128], rhs=qv[:, r, qa * 128:qa * 128 + n],
                                                start=True, stop=True),
                                                reads=[B_kT[kc], B_qsb[qi]], writes=[B_PS[2 * sp_ + half]])
                                        pi = pe_ring.next()
                                        S.op("act", lambda e, sp_=sp_, pi=pi, n=n: e.activation(
                                            out=pexp[pi][:, :, 0:n], in_=PSP[sp_][:, :].rearrange("p (b c) -> p b c", b=2)[:, :, 0:n], func=AF.Exp, scale=0.125),
                                            reads=[B_PS[2 * sp_], B_PS[2 * sp_ + 1]], writes=[B_pexp[pi]])
                                        ti = pt_ring.next()
                                        S.op(mul_ring.next(), lambda e, pi=pi, ti=ti, n=n, ha=ha, hb=hb, e0=e0: e.tensor_tensor(
                                            out=Pt[ti][:, :, 0:n], in0=pexp[pi][:, :, 0:n], in1=Et[:, ha:hb + 1:hb - ha, e0:e0 + n], op=ALU.mult),
                                            reads=[B_pexp[pi], B_E], writes=[B_Pt[ti]])

                                        def pv(oa=oa, ob=ob, c0=c0, n=n, vap=vap, ti=ti, first=first, last=last, g=g):
                                            for half, po in enumerate((oa, ob)):
                                                S.op("pe", lambda e, half=half, po=po: e.matmul(
                                                    PS[po][:, c0:c0 + n], lhsT=vap[:, half, :], rhs=Pt[ti][:, half, 0:n], start=first, stop=last,
                                                    skip_group_check=True),
                                                    reads=[B_vx[g], B_Pt[ti]], writes=[B_PS[po]])
                                        defer(pv, True)
                                        tick()

                                    def seg_end(oa=oa, ob=ob, mi=mi, dil=dil, cls=list(cls), us=list(us), ha=ha, hb=hb, ai=ai):
                                        for half, po in enumerate((oa, ob)):
                                            if typeA:
                                                up = slice(half * 64, half * 64 + 64)
                                                dp = slice(64 - half * 64, 128 - half * 64)
                                                hh = (ha, hb)[half]
                                                q0 = us[0][1] * 128
                                                ri = rd_ring.next()
                                                S.op("act", lambda e, ri=ri, dp=dp, po=po, hh=hh: e.activation(out=rd[ri][dp, :], in_=PS[po][dp, :], func=AF.Ln,
                                                                                   bias=esink[dp, hh:hh + 1], scale=1.0),
                                                     reads=[B_PS[po], B_const], writes=[B_rd[ri]])
                                                S.op("act", lambda e, ri=ri, dp=dp: e.activation(out=rd[ri][dp, :], in_=rd[ri][dp, :], func=AF.Exp, scale=-1.0),
                                                     reads=[B_rd[ri]], writes=[B_rd[ri]])
                                                S.op("dve", lambda e, ai=ai, up=up, q0=q0, po=po, ri=ri, dp=dp: e.tensor_tensor(out=aout[ai][up, q0:q0 + 512], in0=PS[po][up, :], in1=rd[ri][dp, :], op=ALU.mult),
                                                     reads=[B_PS[po], B_rd[ri]], writes=[B_aout[ai]])
                                                continue
                                            stv = stage[mi][half][:].rearrange("p (i r) -> p r i", r=dil)
                                            if len(cls) == 1:
                                                r = cls[0]
                                                q0 = us[0][1] * 128
                                                S.op("dve", lambda e, stv=stv, r=r, q0=q0, po=po: e.tensor_copy(out=stv[:, r, q0:q0 + 512], in_=PS[po][:, :]),
                                                     reads=[B_PS[po]], writes=[B_stage[mi][half]])
                                            else:
                                                r0 = cls[0]
                                                S.op("dve", lambda e, stv=stv, r0=r0, po=po: e.tensor_copy(
                                                    out=stv[:, r0:r0 + 4, 0:128], in_=PS[po][:, :].rearrange("p (u i) -> p u i", i=128)),
                                                    reads=[B_PS[po]], writes=[B_stage[mi][half]])
                                    defer(seg_end, False)

                            def combine(cset=cset, outs=list(outs), T0=T0, tl=list(tl)):
                                if not typeA:
                                    for half in range(2):
                                        up = slice(half * 64, half * 64 + 64)
                                        dp = slice(64 - half * 64, 128 - half * 64)
                                        S.op("dve", lambda e, half=half, dp=dp: e.tensor_tensor(
                                            out=dsum[half][dp, :], in0=stage[0][half][dp, :], in1=stage[1][half][dp, :], op=ALU.add),
                                            reads=[B_stage[0][half], B_stage[1][half]], writes=[B_dsum[half]])
                                        S.op("dve", lambda e, half=half, dp=dp: e.tensor_tensor(
                                            out=dsum[half][dp, :], in0=dsum[half][dp, :], in1=stage[2][half][dp, :], op=ALU.add),
                                            reads=[B_dsum[half], B_stage[2][half]], writes=[B_dsum[half]])
                                        S.op("act", lambda e, half=half, dp=dp: e.activation(out=dsum[half][dp, :], in_=dsum[half][dp, :], func=AF.Ln),
                                             reads=[B_dsum[half]], writes=[B_dsum[half]])
                                        S.op("act", lambda e, half=half, dp=dp: e.activation(out=dsum[half][dp, :], in_=dsum[half][dp, :], func=AF.Exp, scale=-1.0),
                                             reads=[B_dsum[half]], writes=[B_dsum[half]])
                                        S.op("pool", lambda e, half=half, dp=dp, up=up: e.dma_start(out=dsh[up, :], in_=dsum[half][dp, :]),
                                             reads=[B_dsum[half]], writes=[B_dsh], dma="dsh")
                                        for mi, m in enumerate(cset):
                                            ai = outs[mi]
                                            S.op("dve", lambda e, half=half, up=up, mi=mi, ai=ai: e.tensor_tensor(
                                                out=aout[ai][up, :], in0=stage[mi][half][up, :], in1=dsh[up, :], op=ALU.mult),
                                                reads=[B_stage[mi][half], B_dsh], writes=[B_aout[ai]])
                                for mi, m in enumerate(cset):
                                    ai = outs[mi]
                                    S.op("pool", lambda e, ai=ai, m=m, T0=T0: e.dma_start(out=aT_d[m, :, T0:T0 + ST], in_=aout[ai][:]),
                                         reads=[B_aout[ai]], writes=[B_a[t] for t in tl], dma="aout%d" % ai)
                            defer(combine, False)
                        drain()
                while conv_steps:
                    conv_steps.pop(0)()
                S.flush(final=(stop_after == (l, 2)))
            if stop_after == (l, 2):
                return nc

            with contextlib.ExitStack() as st:
                wsl = [sb(st, "wsl%d" % i, [128, WCAP], BF16) for i in range(4)]
                B_wsl = bufs(4, "wsl")
                aT2 = [sb(st, "aT%d" % i, [128, nq, TT], BF16) for i in range(2)]
                B_aT2 = bufs(2, "aTsb")
                xs2 = [sb(st, "xr%d" % i, [128, 8, TT], F32) for i in range(2)]
                B_xs2 = [bufs(8, "xr%d_" % i) for i in range(2)]
                pin2 = [sb(st, "pin%d" % i, [128, 4, PLE], F32) for i in range(2)]
                B_pin2 = bufs(2, "pin")
                hn = sb(st, "hn", [128, 8, TT], BF16)
                B_hn = bufs(8, "hn_")
                sq = sb(st, "sq3", [128, 8, TT], BF16)
                B_sq = Buf("sq3")
                B_sqc3 = bufs(8, "sqc3")
                lnv = sb(st, "lnv3", [128, TT], F32)
                B_lnv = Buf("lnv3")
                hh_t = sb(st, "hh", [128, NJ, TT], BF16)
                B_hh = bufs(NJ, "hh_")
                sg = [sb(st, "sg%d" % i, [128, TT], F32) for i in range(2)]
                B_sg = bufs(2, "sg")
                t2 = [sb(st, "t2%d" % i, [128, TT], F32) for i in range(2)]
                B_t2 = bufs(2, "t2")
                pT = sb(st, "pT", [128, 2, TT], BF16)
                B_pT = bufs(2, "pT")
                if l == 1:
                    yo2 = [sb(st, "yo%d" % i, [128, 4, D], F32) for i in range(2)]
                    B_yo2 = [bufs(4, "yo%d_" % i) for i in range(2)]
                dmy = sb(st, "dmy", [128, 2], F32)
                B_dmy = Buf("dmy")
                S.op("pool", lambda e: e.memset(dmy[:], 1.0), writes=[B_dmy])

                def preload_ln():
                    S.op("act", lambda e: e.activation(out=dmy[:, 1:2], in_=dmy[:, 0:1], func=AF.Ln), reads=[B_dmy], writes=[B_dmy])
                blks = wblocks(l, cfg)[:-1]
                wpp = sb(st, "wpp", [128, 2048], BF16)
                B_wpp = Buf("wpp")
                S.op("sp", lambda e: e.dma_start(out=wpp[:], in_=wsc[l * NBLK + NBLK - 1, :, 0:2048]),
                     reads=[B_wsc[l * NBLK + NBLK - 1]], writes=[B_wpp], dma="wpp")
                wring = Ring([0, 1, 2, 3])
                ps_r = Ring([0, 1, 2, 3, 4, 5])
                wq = []

                def w_load(bi):
                    slot = wring.next()
                    kind, idx = blks[bi]
                    ne = {"wo": nq * 512, "gu": 16 * 256, "dn": NJ * 256, "pg": 8 * 512, "pp": 2 * 1024}[kind]
                    gb = l * NBLK + bi
                    S.op("sp", lambda e, slot=slot, gb=gb, ne=ne: e.dma_start(out=wsl[slot][:, 0:ne], in_=wsc[gb, :, 0:ne]),
                         reads=[B_wsc[gb]], writes=[B_wsl[slot]], dma="wsl%d" % slot)
                    wq.append(slot)

                def p3_loads(t):
                    slot = t % 2
                    t0 = t * TT
                    S.op("sp", lambda e: e.dma_start(out=aT2[slot][:], in_=aT_d[0:nq, :, t0:t0 + TT].rearrange("c p t -> p c t")),
                         reads=[B_a[t]], writes=[B_aT2[slot]], dma="aTsb%d" % slot)
                    S.op("sp", lambda e: e.dma_start(out=xs2[slot][:], in_=xin_d[:, :, t0:t0 + TT].rearrange("c p t -> p c t")),
                         reads=[B_xin[t]], writes=B_xs2[slot], dma="xr%d" % slot)
                    S.op("sp", lambda e: e.dma_start(out=pin2[slot][:], in_=p_d[l, t0:t0 + TT, :].rearrange("(c p) d -> p c d", p=128)),
                         writes=[B_pin2[slot]], dma="pin%d" % slot)

                seqn = [(t, bi) for t in range(ntile) for bi in range(len(blks))]
                wpos = [0]

                def w_prefetch(upto):
                    while wpos[0] < min(upto, len(seqn)):
                        w_load(seqn[wpos[0]][1])
                        wpos[0] += 1

                used = [0]
                pend_out = []

                def w_next():
                    w_prefetch(used[0] + 4)
                    slot = wq[used[0]]
                    used[0] += 1
                    return slot

                def p3_compute(t, mid_hook):
                    slot = t % 2
                    t0 = t * TT
                    xs, Bx = xs2[slot], B_xs2[slot]
                    aT = aT2[slot]
                    pin = pin2[slot]
                    pendn = []
                    for h2 in range(2):
                        ws = w_next()
                        wv_ = wsl[ws][:, 0:nq * 512].rearrange("p (c n) -> p c n", n=512)
                        for m4 in range(4):
                            m = h2 * 4 + m4
                            pi = ps_r.next()
                            for kc in range(nq):
                                S.op("pe", lambda e, wv_=wv_, kc=kc, m4=m4, pi=pi: e.matmul(PS[pi][:, :], lhsT=wv_[:, kc, m4 * 128:(m4 + 1) * 128], rhs=aT[:, kc, :],
                                                                                           start=(kc == 0), stop=(kc == nq - 1)),
                                     reads=[B_wsl[ws], B_aT2[slot]], writes=[B_PS[pi]])
                            S.op("dve", lambda e, m=m, pi=pi: e.tensor_tensor(out=xs[:, m, :], in0=xs[:, m, :], in1=PS[pi][:, :], op=ALU.add),
                                 reads=[Bx[m], B_PS[pi]], writes=[Bx[m]])
                            norm_sq(xs, Bx, sq, B_sqc3, m)
                            pendn.append(m)
                            if len(pendn) > 1:
                                norm_mm(sq, B_sqc3, 6, pendn.pop(0))
                    while pendn:
                        norm_mm(sq, B_sqc3, 6, pendn.pop(0))
                    while pend_out:
                        pend_out.pop(0)()
                    norm_fin(xs, Bx, gffn[:, l, :], hn, B_hn, lnv, B_lnv, 6)
                    for jj in range(NJ // 2):
                        ws = w_next()
                        if "ffn" in DBG_SKIP:
                            continue
                        wv_ = wsl[ws][:, 0:16 * 256].rearrange("p (c n) -> p c n", n=256)
                        for j2 in range(2):
                            j = jj * 2 + j2
                            pg_ = ps_r.next()
                            for kc in range(8):
                                S.op("pe", lambda e, wv_=wv_, kc=kc, j2=j2, pg_=pg_: e.matmul(PS[pg_][:, :], lhsT=wv_[:, kc, j2 * 128:(j2 + 1) * 128], rhs=hn[:, kc, :],
                                                                                             start=(kc == 0), stop=(kc == 7)),
                                     reads=[B_wsl[ws], B_hn[kc]], writes=[B_PS[pg_]])
                            pu_ = ps_r.next()
                            for kc in range(8):
                                S.op("pe", lambda e, wv_=wv_, kc=kc, j2=j2, pu_=pu_: e.matmul(PS[pu_][:, :], lhsT=wv_[:, 8 + kc, j2 * 128:(j2 + 1) * 128], rhs=hn[:, kc, :],
                                                                                             start=(kc == 0), stop=(kc == 7)),
                                     reads=[B_wsl[ws], B_hn[kc]], writes=[B_PS[pu_]])
                            si = j % 2
                            S.op("act", lambda e, si=si, pg_=pg_: e.activation(out=sg[si][:], in_=PS[pg_][:, :], func=AF.Silu),
                                 reads=[B_PS[pg_]], writes=[B_sg[si]])
                            S.op("dve", lambda e, si=si, pu_=pu_, j=j: e.tensor_tensor(out=hh_t[:, j, :], in0=sg[si][:], in1=PS[pu_][:, :], op=ALU.mult),
                                 reads=[B_sg[si], B_PS[pu_]], writes=[B_hh[j]])
                    preload_ln()
                    mid_hook()
                    for mp in range(4):
                        ws = w_next()
                        if "ffn" in DBG_SKIP:
                            continue
                        wv_ = wsl[ws][:, 0:NJ * 256].rearrange("p (c n) -> p c n", n=256)
                        for m2 in range(2):
                            m = mp * 2 + m2
                            pi = ps_r.next()
                            for j in range(NJ):
                                S.op("pe", lambda e, wv_=wv_, j=j, m2=m2, pi=pi: e.matmul(PS[pi][:, :], lhsT=wv_[:, j, m2 * 128:(m2 + 1) * 128], rhs=hh_t[:, j, :],
                                                                                         start=(j == 0), stop=(j == NJ - 1)),
                                     reads=[B_wsl[ws], B_hh[j]], writes=[B_PS[pi]])
                            S.op("dve", lambda e, m=m, pi=pi: e.tensor_tensor(out=xs[:, m, :], in0=xs[:, m, :], in1=PS[pi][:, :], op=ALU.add),
                                 reads=[Bx[m], B_PS[pi]], writes=[Bx[m]])
                            if "ffn" not in DBG_SKIP:
                                norm_sq(xs, Bx, sq, B_sqc3, m)
                                pendn.append(m)
                                if len(pendn) > 1:
                                    norm_mm(sq, B_sqc3, 6, pendn.pop(0))
                    while pendn:
                        norm_mm(sq, B_sqc3, 6, pendn.pop(0))
                    for k2 in range(2):
                        pi = ps_r.next()
                        for tc in range(4):
                            S.op("pe", lambda e, k2=k2, tc=tc, pi=pi: e.transpose(out=PS[pi][:, tc * 128:(tc + 1) * 128], in_=pin[:, tc, k2 * 128:(k2 + 1) * 128],
                                                                                identity=ident[:]),
                                 reads=[B_pin2[slot], B_const], writes=[B_PS[pi]])
                        S.op("act", lambda e, k2=k2, pi=pi: e.copy(out=pT[:, k2, :], in_=PS[pi][:, :]), reads=[B_PS[pi]], writes=[B_pT[k2]])
                    if "ffn" in DBG_SKIP:
                        rmsnorm(xs, Bx, gple[:, l, :], hn, B_hn, sq, B_sq, lnv, B_lnv, 6)
                    else:
                        norm_fin(xs, Bx, gple[:, l, :], hn, B_hn, lnv, B_lnv, 6)
                    wpv = wpp[:, 0:2048].rearrange("p (c n) -> p c n", n=1024)
                    ws_g = None
                    for m in range(8):
                        if m % 4 == 0:
                            ws_g = w_next()
                        if "ple" in DBG_SKIP:
                            continue
                        wgv = wsl[ws_g][:, 0:8 * 512].rearrange("p (c n) -> p c n", n=512)
                        m4 = m % 4
                        pg_ = ps_r.next()
                        for kc in range(8):
                            S.op("pe", lambda e, wgv=wgv, kc=kc, m4=m4, pg_=pg_: e.matmul(PS[pg_][:, :], lhsT=wgv[:, kc, m4 * 128:(m4 + 1) * 128], rhs=hn[:, kc, :],
                                                                                         start=(kc == 0), stop=(kc == 7)),
                                 reads=[B_wsl[ws_g], B_hn[kc]], writes=[B_PS[pg_]])
                        pp_ = ps_r.next()
                        for k2 in range(2):
                            S.op("pe", lambda e, k2=k2, m=m, pp_=pp_: e.matmul(PS[pp_][:, :], lhsT=wpv[:, k2, m * 128:(m + 1) * 128], rhs=pT[:, k2, :],
                                                                               start=(k2 == 0), stop=(k2 == 1)),
                                 reads=[B_wpp, B_pT[k2]], writes=[B_PS[pp_]])
                        si = m % 2
                        S.op("act", lambda e, si=si, pg_=pg_: e.activation(out=sg[si][:], in_=PS[pg_][:, :], func=AF.Sigmoid),
                             reads=[B_PS[pg_]], writes=[B_sg[si]])
                        S.op("dve", lambda e, si=si, pp_=pp_: e.tensor_tensor(out=t2[si][:], in0=sg[si][:], in1=PS[pp_][:, :], op=ALU.mult),
                             reads=[B_sg[si], B_PS[pp_]], writes=[B_t2[si]])
                        S.op("dve", lambda e, si=si, m=m: e.tensor_tensor(out=xs[:, m, :], in0=xs[:, m, :], in1=t2[si][:], op=ALU.add),
                             reads=[Bx[m], B_t2[si]], writes=[Bx[m]])
                    preload_ln()
                    if l == 0:
                        S.op("pool", lambda e: e.dma_start(out=x1T[:, :, t0:t0 + TT].rearrange("c p t -> p c t"), in_=xs[:]),
                             reads=Bx, writes=[B_x1T[t]], dma="xr_st%d" % slot)
                    else:
                        pend_out.append(lambda: p3_output(t))

                def p3_output(t):
                    slot = t % 2
                    t0 = t * TT
                    xs, Bx = xs2[slot], B_xs2[slot]
                    if True:
                        yo, By = yo2[slot], B_yo2[slot]
                        for tc in range(4):
                            for h2 in range(2):
                                pi = ps_r.next()
                                for c4 in range(4):
                                    c = h2 * 4 + c4
                                    S.op("pe", lambda e, c=c, c4=c4, tc=tc, pi=pi: e.transpose(out=PS[pi][:, c4 * 128:(c4 + 1) * 128],
                                                                                               in_=xs[:, c, tc * 128:(tc + 1) * 128], identity=ident[:]),
                                         reads=[Bx[c], B_const], writes=[B_PS[pi]])
                                if h2 == 0:
                                    S.op("act", lambda e, tc=tc, h2=h2, pi=pi: e.copy(out=yo[:, tc, h2 * 512:(h2 + 1) * 512], in_=PS[pi][:, :]),
                                         reads=[B_PS[pi]], writes=[By[tc]])
                                else:
                                    S.op("dve", lambda e, tc=tc, h2=h2, pi=pi: e.tensor_copy(out=yo[:, tc, h2 * 512:(h2 + 1) * 512], in_=PS[pi][:, :]),
                                         reads=[B_PS[pi]], writes=[By[tc]])
                        S.op("pool", lambda e: e.dma_start(out=y_d[t0:t0 + TT, :].rearrange("(c p) d -> p c d", p=128), in_=yo[:]),
                             reads=By, writes=[B_y[t]], dma="yo%d" % slot)

                conv_steps = make_conv_steps(1, st) if l == 0 else []
                per_tile = -(-len(conv_steps) // ntile) if conv_steps else 0
                p3_loads(0)
                for t in range(ntile):
                    def mid_hook(t=t):
                        if t + 1 < ntile:
                            p3_loads(t + 1)
                        for _ in range(per_tile):
                            if conv_steps:
                                conv_steps.pop(0)()
                    p3_compute(t, mid_hook)
                while pend_out:
                    pend_out.pop(0)()
                while conv_steps:
                    conv_steps.pop(0)()
                S.flush(final=(l == 1 or stop_after == (l, 3)))
            if stop_after == (l, 3):
                return nc
        print("total ops", S.total, flush=True)
    return nc


_CACHE = {}


def kernel(x_prompt, x_sample, p_prompt, p_sample, norm_mix, norm_ffn, norm_ple,
           a_wqkv, a_wo, a_q_gain, a_k_gain, a_sink,
           b_wqkv, b_wo, b_q_gain, b_k_gain,
           ffn_w_gate, ffn_w_up, ffn_w_down, ple_w_gate, ple_w_proj):
    f = lambda a: np.ascontiguousarray(np.asarray(a, dtype=np.float32))
    x_prompt, x_sample, p_prompt, p_sample = f(x_prompt), f(x_sample), f(p_prompt), f(p_sample)
    nb, L1, _ = x_prompt.shape
    ns, L2, _ = x_sample.shape
    assert nb == N_CORES and ns == 2 * N_CORES
    seq_lens = [L1, L2, L2]
    key = tuple(seq_lens)
    if key not in _CACHE:
        _CACHE[key] = build(seq_lens)
    nc = _CACHE[key]
    shared = {
        "norm_mix": f(norm_mix), "norm_ffn": f(norm_ffn), "norm_ple": f(norm_ple),
        "a_wqkv": f(a_wqkv)[0], "a_wo": f(a_wo)[0], "a_q_gain": f(a_q_gain)[0], "a_k_gain": f(a_k_gain)[0],
        "a_sink": f(a_sink)[0], "b_wqkv": f(b_wqkv)[0], "b_wo": f(b_wo)[0], "b_q_gain": f(b_q_gain)[0],
        "b_k_gain": f(b_k_gain)[0], "ffn_w_gate": f(ffn_w_gate), "ffn_w_up": f(ffn_w_up),
        "ffn_w_down": f(ffn_w_down), "ple_w_gate": f(ple_w_gate), "ple_w_proj": f(ple_w_proj),
    }
    in_maps = []
    for c in range(N_CORES):
        xc = np.concatenate([x_prompt[c], x_sample[2 * c], x_sample[2 * c + 1]], axis=0)
        pc = np.concatenate([p_prompt[:, c], p_sample[:, 2 * c], p_sample[:, 2 * c + 1]], axis=1)
        m = dict(shared)
        m["x"] = np.ascontiguousarray(xc)
        m["p"] = np.ascontiguousarray(pc)
        in_maps.append(m)
    res = run_bass_kernel_spmd(nc, in_maps, core_ids=list(range(N_CORES)))
    y_prompt = np.empty((nb, L1, D), np.float32)
    y_sample = np.empty((ns, L2, D), np.float32)
    for c in range(N_CORES):
        y = res.results[c]["y"]
        y_prompt[c] = y[0:L1]
        y_sample[2 * c] = y[L1:L1 + L2]
        y_sample[2 * c + 1] = y[L1 + L2:L1 + 2 * L2]
    return (y_prompt, y_sample)
```

```python
import contextlib
import numpy as np
import concourse.bass as bass
import concourse.mybir as mybir
from concourse.bass_utils import run_bass_kernel_spmd

F32 = mybir.dt.float32
BF16 = mybir.dt.bfloat16
I32 = mybir.dt.int32
AF = mybir.ActivationFunctionType
ALU = mybir.AluOpType

D = 1024
FF = 2816
NJ = FF // 128
PLE = 256
PAD = 1024
EPS = 1e-6
ST = 2048
TT = 512
WCAP = 5632
N_CORES = 8
DBG_SKIP = set()


class Buf:
    __slots__ = ("name", "w", "r")

    def __init__(self, name=""):
        self.name = name
        self.w = None
        self.r = []


def bufs(n, name=""):
    return [Buf(name + str(i)) for i in range(n)]


class Op:
    __slots__ = ("eng", "fn", "idx", "is_dma", "semkey", "dmaval", "waits", "target", "cnt", "gidx")


class Sched:
    ENGS = ("pe", "act", "dve", "pool", "sp")

    def __init__(self, nc, st):
        self.nc = nc
        self.ops = {e: [] for e in self.ENGS}
        self.nops = {e: 0 for e in self.ENGS}
        self.cnt = {e: 0 for e in self.ENGS}
        self.dma_cnt = {}
        self.dsem = {}
        self.esem = {e: st.enter_context(nc.semaphore("s_" + e)) for e in self.ENGS}
        self.st = st
        self.seen_eng = {e: {} for e in self.ENGS}
        self.seen_dma = {e: {} for e in self.ENGS}
        self.last_target = {e: None for e in self.ENGS}
        self.total = 0

    def op(self, eng, fn, reads=(), writes=(), dma=None):
        o = Op()
        o.eng = eng
        o.fn = fn
        o.gidx = self.nops[eng]
        self.nops[eng] += 1
        o.is_dma = dma is not None
        o.semkey = dma
        o.target = False
        o.cnt = None
        waits = []
        if o.is_dma:
            if dma not in self.dsem:
                self.dsem[dma] = self.st.enter_context(self.nc.semaphore("d%d" % len(self.dsem)))
                self.dma_cnt[dma] = 0
            self.dma_cnt[dma] += 16
            o.dmaval = self.dma_cnt[dma]
        for b in reads:
            d = b.w
            if d is not None:
                self._dep(o, d, "raw", waits)
        for b in writes:
            d = b.w
            if d is not None:
                self._dep(o, d, "waw", waits)
            for r in b.r:
                self._dep(o, r, "war", waits)
        o.waits = waits
        for b in reads:
            b.r.append(o)
        for b in writes:
            b.w = o
            b.r = []
        self.ops[eng].append(o)
        self.total += 1
        return o

    def _dep(self, o, d, kind, waits):
        if d is o:
            return
        if d.is_dma:
            if o.is_dma and d.semkey == o.semkey and kind == "waw":
                return
            waits.append(d)
        else:
            if d.eng == o.eng and not o.is_dma:
                if o.eng == "pe":
                    return
            waits.append(d)

    def flush(self, final=False):
        nc = self.nc
        plan = {}
        for e in self.ENGS:
            seen_eng = self.seen_eng[e]
            seen_dma = self.seen_dma[e]
            for o in self.ops[e]:
                need_eng = {}
                need_dma = {}
                for d in o.waits:
                    if d.is_dma:
                        if d.dmaval > seen_dma.get(d.semkey, 0):
                            need_dma[d.semkey] = max(need_dma.get(d.semkey, 0), d.dmaval)
                    else:
                        if d.gidx > seen_eng.get(d.eng, -1):
                            if d.eng not in need_eng or need_eng[d.eng].gidx < d.gidx:
                                need_eng[d.eng] = d
                for k, v in need_dma.items():
                    seen_dma[k] = v
                for k, d in need_eng.items():
                    seen_eng[k] = d.gidx
                    d.target = True
                o.waits = (list(need_eng.values()), list(need_dma.items()))
        barrier = {}
        for e in self.ENGS:
            last = None
            for o in self.ops[e]:
                if not o.is_dma:
                    last = o
            if last is not None:
                last.target = True
            c = self.cnt[e]
            for o in self.ops[e]:
                if o.target and not o.is_dma:
                    c += 1
                    o.cnt = c
            self.cnt[e] = c
        end_cnt = dict(self.cnt)
        end_dma = dict(self.dma_cnt)
        prev = getattr(self, "_prev_end", None)
        esem, dsem = self.esem, self.dsem
        ops = self.ops

        def run(engname, eng):
            if prev is not None:
                pc, pd = prev
                for e2, v in pc.items():
                    if e2 != engname and v > 0:
                        eng.wait_ge(esem[e2], v)
                for k, v in pd.items():
                    eng.wait_ge(dsem[k], v)
            for o in ops[engname]:
                we, wd = o.waits
                for d in we:
                    eng.wait_ge(esem[d.eng], d.cnt)
                for k, v in wd:
                    eng.wait_ge(dsem[k], v)
                ins = o.fn(eng)
                if o.is_dma:
                    ins.then_inc(dsem[o.semkey], 16)
                elif o.target:
                    ins.then_inc(esem[engname], 1)
            if final:
                for k, v in end_dma.items():
                    eng.wait_ge(dsem[k], v)

        with nc.Block() as block:
            @block.tensor
            def _(e):
                run("pe", e)

            @block.scalar
            def _(e):
                run("act", e)

            @block.vector
            def _(e):
                run("dve", e)

            @block.gpsimd
            def _(e):
                run("pool", e)

            @block.sync
            def _(e):
                run("sp", e)

        self._prev_end = (end_cnt, end_dma)
        for e in self.ENGS:
            self.seen_eng[e] = {e2: self.nops[e2] - 1 for e2 in self.ENGS}
            self.seen_dma[e] = dict(end_dma)
        self.ops = {e: [] for e in self.ENGS}


class Ring:
    def __init__(self, items):
        self.items = list(items)
        self.i = 0

    def next(self):
        it = self.items[self.i % len(self.items)]
        self.i += 1
        return it


def layer_cfg(l):
    c = {}
    if l == 0:
        c["hq"], c["hkv"] = 16, 4
        c["pairs"] = [(8 * jc + i, 8 * jc + 4 + i) for jc in range(2) for i in range(4)]
        c["kch"] = [m // 4 for m in range(8)]
        c["sets"] = [[m] for m in range(8)]
        c["dil"] = [1] * 8
        c["typeA"] = True
        c["radius"] = 128
        c["wo_groups"] = [(8 * jc, 8 * jc + 4, 4) for jc in range(2)]
    else:
        c["hq"], c["hkv"] = 18, 6
        c["pairs"] = [(6 * g + i, 6 * g + 3 + i) for g in range(3) for i in range(3)]
        c["kch"] = [m // 3 for m in range(9)]
        c["sets"] = [[i, 3 + i, 6 + i] for i in range(3)]
        c["dil"] = [4 ** (m // 3) for m in range(9)]
        c["typeA"] = False
        c["radius"] = 64
        c["wo_groups"] = [(6 * g, 6 * g + 3, 3) for g in range(3)]
    c["nq"] = len(c["pairs"])
    c["nkc"] = c["hkv"] // 2
    c["slopes"] = [2.0 ** (-8.0 * (h + 1) / c["hq"]) for h in range(c["hq"])]
    return c


def wblocks(l, cfg):
    blks = [("wo", 0), ("wo", 1)]
    blks += [("gu", jj) for jj in range(NJ // 2)]
    blks += [("dn", mp) for mp in range(4)]
    blks += [("pg", 0), ("pg", 1), ("pp", 0)]
    return blks


def build(seq_lens, debug=False, stop_after=None):
    nc = bass.Bass("TRN2", target_bir_lowering=False)
    NT = sum(seq_lens)
    nseq = len(seq_lens)
    NTP = NT + PAD * (nseq + 1)
    offs = [sum(seq_lens[:i]) for i in range(nseq)]
    poffs = [offs[i] + PAD * (i + 1) for i in range(nseq)]

    def dt_in(name, shape):
        return nc.dram_tensor(name, shape, F32, kind="ExternalInput").ap()

    def dt_scr(name, shape, dt):
        return nc.dram_tensor(name, shape, dt, kind=("ExternalOutput" if debug else "Internal")).ap()

    x_d = dt_in("x", [NT, D])
    p_d = dt_in("p", [2, NT, PLE])
    norm_mix = dt_in("norm_mix", [2, D])
    norm_ffn = dt_in("norm_ffn", [2, D])
    norm_ple = dt_in("norm_ple", [2, D])
    a_wqkv = dt_in("a_wqkv", [D, 1536])
    a_wo = dt_in("a_wo", [1024, D])
    a_qg = dt_in("a_q_gain", [64])
    a_kg = dt_in("a_k_gain", [64])
    a_sink = dt_in("a_sink", [16])
    b_wqkv = dt_in("b_wqkv", [D, 1920])
    b_wo = dt_in("b_wo", [1152, D])
    b_qg = dt_in("b_q_gain", [64])
    b_kg = dt_in("b_k_gain", [64])
    w_gate = dt_in("ffn_w_gate", [2, D, FF])
    w_up = dt_in("ffn_w_up", [2, D, FF])
    w_down = dt_in("ffn_w_down", [2, FF, D])
    w_pg = dt_in("ple_w_gate", [2, D, D])
    w_pp = dt_in("ple_w_proj", [2, PLE, D])
    y_d = nc.dram_tensor("y", [NT, D], F32, kind="ExternalOutput").ap()

    wqkv_d = [a_wqkv, b_wqkv]
    wo_d = [a_wo, b_wo]
    qg_d = [a_qg, b_qg]
    kg_d = [a_kg, b_kg]

    NBLK = 20
    wsc = dt_scr("wsc", [2 * NBLK, 128, WCAP], BF16)
    xT0 = dt_scr("xT0", [8, 128, NT], F32)
    x1T = dt_scr("x1T", [8, 128, NT], F32)
    qT_d = dt_scr("qT", [9, 128, NT], BF16)
    kT_d = dt_scr("kT", [3, 128, NTP], BF16)
    vx_d = dt_scr("vx", [NTP, 6, 128], BF16)
    aT_d = dt_scr("aT", [9, 128, NT], BF16)

    ntile = NT // TT
    B_xT0 = bufs(ntile, "xT0_")
    B_x1T = bufs(ntile, "x1T_")
    B_q = bufs(ntile, "q_")
    B_k = bufs(ntile, "k_")
    B_v = bufs(ntile, "v_")
    B_a = bufs(ntile, "a_")
    B_y = bufs(ntile, "y_")
    B_wsc = bufs(2 * NBLK, "wsc_")
    B_pad = Buf("pad")

    def tile_pt0(t):
        t0 = t * TT
        for s in range(nseq):
            if offs[s] <= t0 < offs[s] + seq_lens[s]:
                return poffs[s] + (t0 - offs[s])
        raise AssertionError

    with contextlib.ExitStack() as gst:
        S = Sched(nc, gst)

        uniq = [0]

        def sb(st, name, shape, dt):
            uniq[0] += 1
            return st.enter_context(nc.sbuf_tensor("%s_%d" % (name, uniq[0]), shape, dt))

        def psum(st, name):
            return st.enter_context(nc.psum_tensor(name, [128, 512], F32))

        ident = sb(gst, "ident", [128, 128], F32)
        ones_bf = sb(gst, "ones_bf", [128, 128], BF16)
        blk_bf = sb(gst, "blk_bf", [128, 128], BF16)
        gmix = sb(gst, "gmix", [128, 2, 8], F32)
        gffn = sb(gst, "gffn", [128, 2, 8], F32)
        gple = sb(gst, "gple", [128, 2, 8], F32)
        gq = sb(gst, "gq", [128, 2], F32)
        gk = sb(gst, "gk", [128, 2], F32)
        esink = sb(gst, "esink", [128, 16], F32)
        B_const = Buf("const")

        PSP = [gst.enter_context(nc.psum_tensor("psp%d" % i, [128, 1024], F32)) for i in range(4)]
        PS = [PSP[i // 2][:, (i % 2) * 512:(i % 2 + 1) * 512] for i in range(8)]
        B_PS = bufs(8, "ps")

        S.op("pool", lambda e: e.memset(ident[:], 0.0), writes=[B_const])
        S.op("pool", lambda e: e.affine_select(out=ident[:], in_=ident[:], pattern=[[-1, 128]],
                                               compare_op=ALU.not_equal, fill=1.0, base=0,
                                               channel_multiplier=1), reads=[B_const], writes=[B_const])
        S.op("pool", lambda e: e.memset(ones_bf[:], 1.0), writes=[B_const])
        S.op("pool", lambda e: e.memset(blk_bf[:], 0.0), writes=[B_const])
        S.op("pool", lambda e: e.memset(blk_bf[0:64, 0:64], 1.0), writes=[B_const])
        S.op("pool", lambda e: e.memset(blk_bf[64:128, 64:128], 1.0), writes=[B_const])
        for l in range(2):
            for (g_sb, g_d) in ((gmix, norm_mix), (gffn, norm_ffn), (gple, norm_ple)):
                S.op("sp", lambda e, g_sb=g_sb, g_d=g_d, l=l: e.dma_start(
                    out=g_sb[:, l, :], in_=g_d[l].rearrange("(c p) -> p c", p=128),
                    allow_slow_non_contiguous=True), writes=[B_const], dma="const")
            for half in range(2):
                S.op("sp", lambda e, l=l, half=half: e.dma_start(
                    out=gq[half * 64:(half + 1) * 64, l:l + 1], in_=qg_d[l].rearrange("(p o) -> p o", o=1),
                    allow_slow_non_contiguous=True), writes=[B_const], dma="const")
                S.op("sp", lambda e, l=l, half=half: e.dma_start(
                    out=gk[half * 64:(half + 1) * 64, l:l + 1], in_=kg_d[l].rearrange("(p o) -> p o", o=1),
                    allow_slow_non_contiguous=True), writes=[B_const], dma="const")
        S.op("sp", lambda e: e.dma_start(out=esink[:], in_=a_sink.partition_broadcast(128),
                                         allow_slow_non_contiguous=True), writes=[B_const], dma="const")
        S.op("act", lambda e: e.activation(out=esink[:], in_=esink[:], func=AF.Exp), reads=[B_const], writes=[B_const])

        with contextlib.ExitStack() as st:
            zt = sb(st, "zt", [128, 2048], BF16)
            B_zt = Buf("zt")
            S.op("pool", lambda e: e.memset(zt[:], 0.0), writes=[B_zt])
            pad_starts = [0] + [poffs[i] + seq_lens[i] for i in range(nseq)]
            for ps0 in pad_starts:
                for kc in range(3):
                    S.op("pool", lambda e, ps0=ps0, kc=kc: e.dma_start(out=kT_d[kc, :, ps0:ps0 + PAD], in_=zt[:, 0:PAD]),
                         reads=[B_zt], writes=[B_pad], dma="zt")
                for q4 in range(4):
                    S.op("pool", lambda e, ps0=ps0, q4=q4: e.dma_start(
                        out=vx_d[ps0 + q4 * 256:ps0 + (q4 + 1) * 256].rearrange("(p a) h c -> p (a h c)", p=128),
                        in_=zt[:, 0:1536]), reads=[B_zt], writes=[B_pad], dma="zt")
            S.flush()

        HCAP = 2816

        def make_conv_steps(l, st):
            wst = [sb(st, "cwst%d" % i, [128, HCAP], F32) for i in range(2)]
            wbf = [sb(st, "cwbf%d" % i, [128, HCAP], BF16) for i in range(2)]
            B_wst = bufs(2, "cwst")
            B_wbf = bufs(2, "cwbf")
            steps = []
            k = 0
            cfg = layer_cfg(l)
            for bi, (kind, idx) in enumerate(wblocks(l, cfg)):
                nch, ncol = {"wo": (cfg["nq"], 512), "gu": (16, 256), "dn": (NJ, 256), "pg": (8, 512), "pp": (2, 1024)}[kind]
                hc = (nch + 1) // 2
                for (c0, c1) in ((0, hc), (hc, nch)):
                    slot = k % 2
                    k += 1

                    def step(bi=bi, kind=kind, idx=idx, slot=slot, c0=c0, c1=c1, ncol=ncol):
                        ne = (c1 - c0) * ncol
                        dstv = wst[slot][:, 0:ne].rearrange("p (c n) -> p c n", n=ncol)
                        dmas = []
                        if kind == "wo":
                            wo_h = wo_d[l].rearrange("(h d) n -> d h n", d=64)
                            m0 = 0
                            for (a0, b0, n) in cfg["wo_groups"]:
                                lo, hi = max(c0, m0), min(c1, m0 + n)
                                if lo < hi:
                                    dmas.append((dstv[0:64, lo - c0:hi - c0, :], wo_h[:, a0 + lo - m0:a0 + hi - m0, idx * 512:(idx + 1) * 512]))
                                    dmas.append((dstv[64:128, lo - c0:hi - c0, :], wo_h[:, b0 + lo - m0:b0 + hi - m0, idx * 512:(idx + 1) * 512]))
                                m0 += n
                        elif kind == "gu":
                            src = (w_gate if c0 == 0 else w_up)[l].rearrange("(c p) n -> p c n", p=128)
                            dmas.append((dstv, src[:, :, idx * 256:(idx + 1) * 256]))
                        elif kind == "dn":
                            dmas.append((dstv, w_down[l].rearrange("(c p) n -> p c n", p=128)[:, c0:c1, idx * 256:(idx + 1) * 256]))
                        elif kind == "pg":
                            dmas.append((dstv, w_pg[l].rearrange("(c p) n -> p c n", p=128)[:, c0:c1, idx * 512:(idx + 1) * 512]))
                        else:
                            dmas.append((dstv, w_pp[l].rearrange("(c p) n -> p c n", p=128)[:, c0:c1, :]))
                        for (dst, src) in dmas:
                            S.op("sp", lambda e, dst=dst, src=src: e.dma_start(out=dst, in_=src),
                                 writes=[B_wst[slot]], dma="cwst%d" % slot)
                        S.op("pool", lambda e: e.tensor_copy(out=wbf[slot][:, 0:ne], in_=wst[slot][:, 0:ne]),
                             reads=[B_wst[slot]], writes=[B_wbf[slot]])
                        gb = l * NBLK + bi
                        S.op("pool", lambda e: e.dma_start(out=wsc[gb, :, c0 * ncol:c1 * ncol], in_=wbf[slot][:, 0:ne]),
                             reads=[B_wbf[slot]], writes=[B_wsc[gb]], dma="cwbf%d" % slot)
                    steps.append(step)
            return steps

        def norm_sq(xs, B_xs, sq, B_sqc, c, eng="act"):
            if eng == "act":
                S.op("act", lambda e: e.activation(out=sq[:, c, :], in_=xs[:, c, :], func=AF.Square), reads=[B_xs[c]], writes=[B_sqc[c]])
            else:
                S.op(eng, lambda e: e.tensor_tensor(out=sq[:, c, :], in0=xs[:, c, :], in1=xs[:, c, :], op=ALU.mult), reads=[B_xs[c]], writes=[B_sqc[c]])

        def norm_mm(sq, B_sqc, ps_i, c):
            S.op("pe", lambda e: e.matmul(PS[ps_i][:, :], lhsT=ones_bf[:], rhs=sq[:, c, :], start=(c == 0), stop=(c == 7)),
                 reads=[B_sqc[c], B_const], writes=[B_PS[ps_i]])

        def norm_rstd(lnv, B_lnv, ps_i):
            S.op("act", lambda e: e.activation(out=lnv[:], in_=PS[ps_i][:, :], func=AF.Ln, scale=1.0 / D, bias=EPS),
                 reads=[B_PS[ps_i]], writes=[B_lnv])
            S.op("act", lambda e: e.activation(out=lnv[:], in_=lnv[:], func=AF.Exp, scale=-0.5), reads=[B_lnv], writes=[B_lnv])

        def norm_apply(xs, B_xs, gcol, hn, B_hn, lnv, B_lnv, c):
            S.op("dve", lambda e: e.scalar_tensor_tensor(out=hn[:, c, :], in0=xs[:, c, :], scalar=gcol[:, c:c + 1],
                                                          in1=lnv[:], op0=ALU.mult, op1=ALU.mult),
                 reads=[B_xs[c], B_lnv, B_const], writes=[B_hn[c]])

        def norm_fin(xs, B_xs, gcol, hn, B_hn, lnv, B_lnv, ps_i):
            norm_rstd(lnv, B_lnv, ps_i)
            for c in range(8):
                norm_apply(xs, B_xs, gcol, hn, B_hn, lnv, B_lnv, c)

        def rmsnorm(xs, B_xs, gcol, hn, B_hn, sq, B_sq, lnv, B_lnv, ps_i):
            S.op("act", lambda e: e.activation(out=sq[:], in_=xs[:], func=AF.Square), reads=B_xs, writes=[B_sq])
            for c in range(8):
                S.op("pe", lambda e, c=c: e.matmul(PS[ps_i][:, :], lhsT=ones_bf[:], rhs=sq[:, c, :], start=(c == 0), stop=(c == 7)),
                     reads=[B_sq, B_const], writes=[B_PS[ps_i]])
            S.op("act", lambda e: e.activation(out=lnv[:], in_=PS[ps_i][:, :], func=AF.Ln, scale=1.0 / D, bias=EPS),
                 reads=[B_PS[ps_i]], writes=[B_lnv])
            S.op("act", lambda e: e.activation(out=lnv[:], in_=lnv[:], func=AF.Exp, scale=-0.5), reads=[B_lnv], writes=[B_lnv])
            for c in range(8):
                S.op("dve", lambda e, c=c: e.scalar_tensor_tensor(out=hn[:, c, :], in0=xs[:, c, :], scalar=gcol[:, c:c + 1],
                                                                  in1=lnv[:], op0=ALU.mult, op1=ALU.mult),
                     reads=[B_xs[c], B_lnv, B_const], writes=[B_hn[c]])

        for l in range(2):
            cfg = layer_cfg(l)
            nq, nkc, hkv, hq = cfg["nq"], cfg["nkc"], cfg["hkv"], cfg["hq"]
            nqk = nq + nkc
            B_xin = B_xT0 if l == 0 else B_x1T
            xin_d = xT0 if l == 0 else x1T
            B_xout = B_x1T if l == 0 else None

            with contextlib.ExitStack() as st:
                wqk = sb(st, "wqk", [128, 8, nqk * 128], BF16)
                wv = sb(st, "wv", [128, 8, hkv * 64], BF16)
                wtmp = [sb(st, "wtmp%d" % i, [128, 8, 128], F32) for i in range(2)]
                B_wtmp = bufs(2, "wtmp")
                B_wqk = Buf("wqk")
                xtok = [sb(st, "xtok%d" % i, [128, 4, D], F32) for i in range(2)] if l == 0 else None
                B_xtok = bufs(2, "xtok")
                xs2 = [sb(st, "xs%d" % i, [128, 8, TT], F32) for i in range(2)]
                B_xs2 = [bufs(8, "xs%d_" % i) for i in range(2)]
                hn2 = [sb(st, "hn%d" % i, [128, 8, TT], BF16) for i in range(2)]
                B_hn2 = [bufs(8, "hn%d_" % i) for i in range(2)]
                sq = sb(st, "sq", [128, 8, TT], BF16)
                B_sq = Buf("sq")
                lnv = sb(st, "lnv", [128, TT], F32)
                B_lnv = Buf("lnv")
                qko2 = [sb(st, "qko%d" % i, [128, nqk, TT], BF16) for i in range(2)]
                B_qko2 = [bufs(nqk, "qko%d_" % i) for i in range(2)]
                vxo2 = [sb(st, "vxo%d" % i, [128, 4, hkv, 128], BF16) for i in range(2)]
                B_vxo2 = [bufs(4, "vxo%d_" % i) for i in range(2)]
                hsq = [sb(st, "hsq%d" % i, [128, TT], BF16) for i in range(2)]
                B_hsq = bufs(2, "hsq")
                hln = [sb(st, "hln%d" % i, [128, TT], F32) for i in range(2)]
                B_hln = bufs(2, "hln")

                wsrc = wqkv_d[l].rearrange("(c p) n -> p c n", p=128)
                pieces = []
                for m, (a, b) in enumerate(cfg["pairs"]):
                    pieces.append((wqk[:, :, m * 128:m * 128 + 64], a * 64, 64))
                    pieces.append((wqk[:, :, m * 128 + 64:m * 128 + 128], b * 64, 64))
                for jc in range(nkc):
                    pieces.append((wqk[:, :, (nq + jc) * 128:(nq + jc + 1) * 128], hq * 64 + jc * 128, 128))
                for h0 in range(0, hkv * 64, 128):
                    pieces.append((wv[:, :, h0:h0 + 128], (hq + hkv) * 64 + h0, 128))
                for i, (dst, c0, n) in enumerate(pieces):
                    slot = i % 2
                    S.op("sp", lambda e, slot=slot, c0=c0, n=n: e.dma_start(out=wtmp[slot][:, :, 0:n], in_=wsrc[:, :, c0:c0 + n]),
                         writes=[B_wtmp[slot]], dma="wtmp%d" % slot)
                    eng = "dve" if i % 2 == 0 else "pool"
                    S.op(eng, lambda e, dst=dst, slot=slot, n=n: e.tensor_copy(out=dst, in_=wtmp[slot][:, :, 0:n]),
                         reads=[B_wtmp[slot]], writes=[B_wqk])
                for i in range(2):
                    S.op("pool", lambda e, i=i: e.memset(vxo2[i][:], 1.0), writes=B_vxo2[i])

                def p1_loads(t):
                    slot = t % 2
                    t0 = t * TT
                    if l == 0:
                        S.op("sp", lambda e: e.dma_start(out=xtok[slot][:], in_=x_d[t0:t0 + TT, :].rearrange("(c p) d -> p c d", p=128)),
                             writes=[B_xtok[slot]], dma="xtok%d" % slot)
                    else:
                        S.op("sp", lambda e: e.dma_start(out=xs2[slot][:], in_=xin_d[:, :, t0:t0 + TT].rearrange("c p t -> p c t")),
                             reads=[B_xin[t]], writes=B_xs2[slot], dma="xs%d" % slot)

                ps_tr = Ring([0, 1])
                ps_q = Ring([3, 4, 5])
                ps_h = Ring([6, 7])

                B_sqc = bufs(8, "sqc")

                def p1_frontA(t):
                    slot = t % 2
                    t0 = t * TT
                    xs, Bx = xs2[slot], B_xs2[slot]
                    if l == 0:
                        for c in range(8):
                            pi = ps_tr.next()
                            for tc in range(4):
                                S.op("pe", lambda e, c=c, tc=tc, pi=pi: e.transpose(
                                    out=PS[pi][:, tc * 128:(tc + 1) * 128], in_=xtok[slot][:, tc, c * 128:(c + 1) * 128], identity=ident[:]),
                                    reads=[B_xtok[slot], B_const], writes=[B_PS[pi]])
                            if c % 2 == 0:
                                S.op("act", lambda e, c=c, pi=pi: e.copy(out=xs[:, c, :], in_=PS[pi][:, :]), reads=[B_PS[pi]], writes=[Bx[c]])
                            else:
                                S.op("dve", lambda e, c=c, pi=pi: e.tensor_copy(out=xs[:, c, :], in_=PS[pi][:, :]), reads=[B_PS[pi]], writes=[Bx[c]])
                        S.op("pool", lambda e: e.dma_start(out=xT0[:, :, t0:t0 + TT].rearrange("c p t -> p c t"), in_=xs[:]),
                             reads=Bx, writes=[B_xT0[t]], dma="xs_st%d" % slot)
                    for c in range(8):
                        norm_sq(xs, Bx, sq, B_sqc, c, eng=("act" if l == 0 else "dve"))

                def p1_frontB(t):
                    for c in range(8):
                        norm_mm(sq, B_sqc, 2, c)
                    norm_rstd(lnv, B_lnv, 2)

                def p1_frontC(t, cs):
                    slot = t % 2
                    xs, Bx = xs2[slot], B_xs2[slot]
                    hn, Bh = hn2[slot], B_hn2[slot]
                    for c in cs:
                        norm_apply(xs, Bx, gmix[:, l, :], hn, Bh, lnv, B_lnv, c)

                def p1_compute(t, hooks):
                    slot = t % 2
                    t0 = t * TT
                    pt0 = tile_pt0(t)
                    xs, Bx = xs2[slot], B_xs2[slot]
                    hn, Bh = hn2[slot], B_hn2[slot]
                    qko, Bq = qko2[slot], B_qko2[slot]
                    vxo, Bv = vxo2[slot], B_vxo2[slot]
                    pend = []
                    for m in range(nqk):
                        if m in hooks:
                            hooks[m]()
                        pi = ps_q.next()
                        for kc in range(8):
                            S.op("pe", lambda e, m=m, kc=kc, pi=pi: e.matmul(PS[pi][:, :], lhsT=wqk[:, kc, m * 128:(m + 1) * 128], rhs=hn[:, kc, :],
                                                                             start=(kc == 0), stop=(kc == 7)),
                                 reads=[B_wqk, Bh[kc]], writes=[B_PS[pi]])
                        hs = m % 2
                        qi3 = m % 3
                        S.op("act", lambda e, pi=pi, hs=hs: e.activation(out=hsq[hs][:], in_=PS[pi][:, :], func=AF.Square),
                             reads=[B_PS[pi]], writes=[B_hsq[hs]])

                        def tail(m=m, pi=pi, hs=hs, qi3=qi3):
                            ph = ps_h.next()
                            S.op("pe", lambda e: e.matmul(PS[ph][:, :], lhsT=blk_bf[:], rhs=hsq[hs][:], start=True, stop=True),
                                 reads=[B_hsq[hs], B_const], writes=[B_PS[ph]])
                            S.op("act", lambda e: e.activation(out=hln[hs][:], in_=PS[ph][:, :], func=AF.Ln, scale=1.0 / 64, bias=EPS),
                                 reads=[B_PS[ph]], writes=[B_hln[hs]])
                            S.op("act", lambda e: e.activation(out=hln[hs][:], in_=hln[hs][:], func=AF.Exp, scale=-0.5),
                                 reads=[B_hln[hs]], writes=[B_hln[hs]])
                            gcol = gq[:, l:l + 1] if m < nq else gk[:, l:l + 1]
                            S.op("dve", lambda e: e.scalar_tensor_tensor(
                                out=qko[:, m, :], in0=PS[pi][:, :], scalar=gcol, in1=hln[hs][:], op0=ALU.mult, op1=ALU.mult),
                                reads=[B_PS[pi], B_hln[hs], B_const], writes=[Bq[m]])
                        pend.append(tail)
                        if len(pend) > 1:
                            pend.pop(0)()
                    while pend:
                        pend.pop(0)()
                    nv = hkv * 64
                    for tc in range(4):
                        pi = ps_tr.next()
                        for kc in range(8):
                            S.op("pe", lambda e, tc=tc, kc=kc, pi=pi: e.matmul(PS[pi][:, 0:nv], lhsT=hn[:, kc, tc * 128:(tc + 1) * 128], rhs=wv[:, kc, :],
                                                                               start=(kc == 0), stop=(kc == 7)),
                                 reads=[B_wqk, Bh[kc]], writes=[B_PS[pi]])
                        pv = PS[pi][:, 0:nv].rearrange("p (h two d) -> p h two d", two=2, d=64)
                        vv = vxo[:, tc, :, :].rearrange("p (h two) c -> p h two c", two=2)
                        S.op("act", lambda e, pv=pv, vv=vv: e.copy(out=vv[:, :, 0, 0:64], in_=pv[:, :, 0, :]), reads=[B_PS[pi]], writes=[Bv[tc]])
                        S.op("dve", lambda e, pv=pv, vv=vv: e.tensor_copy(out=vv[:, :, 1, 64:128], in_=pv[:, :, 1, :]), reads=[B_PS[pi]], writes=[Bv[tc]])
                    S.op("pool", lambda e: e.dma_start(out=qT_d[0:nq, :, t0:t0 + TT].rearrange("c p t -> p c t"), in_=qko[:, 0:nq, :]),
                         reads=Bq[0:nq], writes=[B_q[t]], dma="q_st%d" % slot)
                    S.op("pool", lambda e: e.dma_start(out=kT_d[0:nkc, :, pt0:pt0 + TT].rearrange("c p t -> p c t"), in_=qko[:, nq:nqk, :]),
                         reads=Bq[nq:nqk], writes=[B_k[t]], dma="k_st%d" % slot)
                    S.op("pool", lambda e: e.dma_start(out=vx_d[pt0:pt0 + TT, 0:hkv, :].rearrange("(c p) h d -> p c h d", p=128), in_=vxo[:]),
                         reads=Bv, writes=[B_v[t]], dma="v_st%d" % slot)

                conv_steps = make_conv_steps(0, st) if l == 0 else []
                per_tile = -(-len(conv_steps) // ntile) if conv_steps else 0
                p1_loads(0)
                p1_frontA(0)
                p1_frontB(0)
                p1_frontC(0, range(8))
                for t in range(ntile):
                    for _ in range(per_tile):
                        if conv_steps:
                            conv_steps.pop(0)()
                    hooks = {}
                    if t + 1 < ntile:
                        p1_loads(t + 1)
                        hooks[1] = (lambda t=t: p1_frontA(t + 1))
                        hooks[4] = (lambda t=t: p1_frontB(t + 1))
                        hooks[6] = (lambda t=t: p1_frontC(t + 1, (0, 1)))
                        hooks[7] = (lambda t=t: p1_frontC(t + 1, (2, 3)))
                        hooks[8] = (lambda t=t: p1_frontC(t + 1, (4, 5)))
                        hooks[9] = (lambda t=t: p1_frontC(t + 1, (6, 7)))
                    p1_compute(t, hooks)
                while conv_steps:
                    conv_steps.pop(0)()
                S.flush(final=(stop_after == (l, 1)))
            if stop_after == (l, 1):
                return nc

            with contextlib.ExitStack() as st:
                typeA = cfg["typeA"]
                R = cfg["radius"]
                EW = 384 if typeA else 256
                eoff = 128 if typeA else 64
                Et = sb(st, "Et", [128, hq, EW], BF16)
                Ef = sb(st, "Ef", [128, EW], F32)
                B_E = Buf("E")
                dI = sb(st, "dI", [128, EW], I32)
                dF = sb(st, "dF", [128, EW], F32)
                B_dI = Buf("dI")
                S.op("pool", lambda e: e.iota(dI[:], pattern=[[1, EW]], base=-eoff, channel_multiplier=-1), writes=[B_dI])
                S.op("dve", lambda e: e.tensor_copy(out=dF[:], in_=dI[:]), reads=[B_dI], writes=[B_dI])
                S.op("act", lambda e: e.activation(out=dF[:], in_=dF[:], func=AF.Abs), reads=[B_dI], writes=[B_dI])
                for h in range(hq):
                    dil_h = 1 if typeA else 4 ** (h // 6)
                    sc = -cfg["slopes"][h] * dil_h
                    S.op("act", lambda e, h=h, sc=sc: e.activation(out=Ef[:], in_=dF[:], func=AF.Exp, scale=sc), reads=[B_dI, B_E], writes=[B_E])
                    S.op("pool", lambda e, h=h: e.affine_select(out=Ef[:], in_=Ef[:], pattern=[[-1, EW]], compare_op=ALU.is_ge,
                                                                fill=0.0, base=R + eoff, channel_multiplier=1), reads=[B_E], writes=[B_E])
                    S.op("pool", lambda e, h=h: e.affine_select(out=Ef[:], in_=Ef[:], pattern=[[1, EW]], compare_op=ALU.is_ge,
                                                                fill=0.0, base=R - eoff, channel_multiplier=-1), reads=[B_E], writes=[B_E])
                    S.op("dve", lambda e, h=h: e.tensor_copy(out=Et[:, h, :], in_=Ef[:]), reads=[B_E], writes=[B_E])

                kT_sb = sb(st, "kT_sb", [128, nkc, 4096], BF16)
                B_kT = bufs(nkc, "kTsb")
                if typeA:
                    vx_sb = [sb(st, "vx_sb", [128, 18, 4, 128], BF16)]
                else:
                    vx_sb = [sb(st, "vx_sb%d" % g, [128, 4 ** g, 16 // (4 ** g) + 1, 2, 128], BF16) for g in range(3)]
                B_vx = bufs(len(vx_sb), "vxsb")
                nmem = len(cfg["sets"][0])
                qsb = [[sb(st, "qsb%d_%d" % (i, hf), [128, ST], BF16) for hf in range(2)] for i in range(3)]
                B_qsb = bufs(3, "qsb")
                for i in range(3):
                    S.op("pool", lambda e, i=i: e.memset(qsb[i][0][64:128, :], 0.0), writes=[B_qsb[i]])
                    S.op("pool", lambda e, i=i: e.memset(qsb[i][1][0:64, :], 0.0), writes=[B_qsb[i]])
                stage = [[sb(st, "stg%d_%d" % (i, hh), [128, ST], F32) for hh in range(2)] for i in range(nmem)] if not typeA else None
                B_stage = [[Buf("stg") for hh in range(2)] for i in range(nmem)]
                aout = [sb(st, "aout%d" % i, [128, ST], BF16) for i in range(2 * nmem)]
                B_aout = bufs(2 * nmem, "aout")
                dsum1 = sb(st, "dsum", [128, ST], F32) if not typeA else None
                dsum = [dsum1, dsum1]
                B_dsum = bufs(2, "dsum")
                pexp = [sb(st, "pexp%d" % i, [128, 2, 384], BF16) for i in range(4)]
                B_pexp = bufs(4, "pexp")
                NPT = 8
                LA = 5
                conv_steps = []
                Pt = [sb(st, "Pt%d" % i, [128, 2, 384], BF16) for i in range(NPT)]
                B_Pt = bufs(NPT, "Pt")
                pt_ring = Ring(list(range(NPT)))
                rd = [sb(st, "rd%d" % i, [128, 512], F32) for i in range(2)] if typeA else None
                B_rd = bufs(2, "rd")
                rd_ring = Ring([0, 1])
                ps_S = Ring([0, 1])
                ps_O = Ring([(4, 5), (6, 7)])
                mul_ring = Ring(["dve", "dve", "pool"])
                dsh = sb(st, "dsh", [128, ST], F32) if not typeA else None
                B_dsh = Buf("dsh")
                dq = []

                def defer(fn, is_item):
                    dq.append((fn, is_item))

                def tick():
                    while sum(1 for (_, it) in dq if it) > LA:
                        fn, it = dq.pop(0)
                        fn()

                def drain():
                    while dq:
                        fn, it = dq.pop(0)
                        fn()

                pe_ring = Ring([0, 1, 2, 3])
                q_ring = Ring([0, 1, 2])
                a_ring = Ring(list(range(2 * nmem)))

                for s in range(nseq):
                    for stile in range(seq_lens[s] // ST):
                        T0 = offs[s] + stile * ST
                        PT0 = poffs[s] + stile * ST
                        tl = list(range(T0 // TT, T0 // TT + 4))
                        lo_t = max(offs[s], T0 - PAD) // TT
                        hi_t = (min(offs[s] + seq_lens[s], T0 + ST + PAD) - 1) // TT
                        halo_k = [B_k[t] for t in range(lo_t, hi_t + 1)] + [B_pad]
                        halo_v = [B_v[t] for t in range(lo_t, hi_t + 1)] + [B_pad]
                        for kc in range(nkc):
                            S.op("sp", lambda e, kc=kc, PT0=PT0: e.dma_start(out=kT_sb[:, kc, :], in_=kT_d[kc, :, PT0 - PAD:PT0 + ST + PAD]),
                                 reads=halo_k, writes=[B_kT[kc]], dma="kTsb%d" % kc)
                        if typeA:
                            for b4 in range(3):
                                S.op("sp", lambda e, b4=b4, PT0=PT0: e.dma_start(
                                    out=vx_sb[0][:, b4 * 6:(b4 + 1) * 6, :, :],
                                    in_=vx_d[PT0 - 128 + b4 * 768:PT0 - 128 + (b4 + 1) * 768, 0:4, :].rearrange("(b p) h c -> p b h c", p=128)),
                                    reads=halo_v, writes=[B_vx[0]], dma="vxsb")
                        else:
                            for g in range(3):
                                dil = 4 ** g
                                nqb = 16 // dil
                                for r in range(dil):
                                    base = PT0 - 64 * dil
                                    nb = nqb + 1
                                    src = vx_d[base:base + nb * 128 * dil, 2 * g:2 * g + 2, :].rearrange(
                                        "(b p r) h c -> r p b h c", p=128, r=dil)[r]
                                    bsplit = [(0, nb)] if nb <= 8 else [(0, 6), (6, 12), (12, nb)]
                                    for (b0, b1) in bsplit:
                                        S.op("sp", lambda e, g=g, r=r, src=src, b0=b0, b1=b1: e.dma_start(
                                            out=vx_sb[g][:, r, b0:b1, :, :], in_=src[:, b0:b1, :, :]),
                                            reads=halo_v, writes=[B_vx[g]], dma="vxsb%d" % g)

                        for cset in cfg["sets"]:
                            outs = []
                            for mi, m in enumerate(cset):
                                (ha, hb) = cfg["pairs"][m]
                                kc = cfg["kch"][m]
                                dil = cfg["dil"][m]
                                g = 0 if typeA else m // 3
                                nqb = 16 // dil
                                qi = q_ring.next()
                                for hf in range(2):
                                    S.op("sp", lambda e, qi=qi, m=m, T0=T0, hf=hf: e.dma_start(
                                        out=qsb[qi][hf][hf * 64:hf * 64 + 64, :], in_=qT_d[m, hf * 64:hf * 64 + 64, T0:T0 + ST]),
                                        reads=[B_q[t] for t in tl], writes=[B_qsb[qi]], dma="qsb%d" % qi)
                                qv2 = [qsb[qi][hf][:].rearrange("p (i r) -> p r i", r=dil) for hf in range(2)]
                                if conv_steps:
                                    conv_steps.pop(0)()
                                kv = kT_sb[:, kc, :].rearrange("p (i r) -> p r i", r=dil)
                                kbase = PAD // dil
                                ai = a_ring.next()
                                outs.append(ai)
                                units = [(r, qb) for r in range(dil) for qb in range(nqb)]
                                for sg in range(len(units) // 4):
                                    us = units[sg * 4:sg * 4 + 4]
                                    (oa, ob) = ps_O.next()
                                    jobs = []
                                    cls = []
                                    for (r, qb) in us:
                                        if r not in cls:
                                            cls.append(r)
                                    for r in cls:
                                        qbs = [qb for (r2, qb) in us if r2 == r]
                                        col0 = [i for i, u in enumerate(us) if u[0] == r][0] * 128
                                        if typeA:
                                            brange = range(qbs[0] - 1, qbs[-1] + 2)
                                        else:
                                            brange = range(qbs[0], qbs[-1] + 2)
                                        for b in brange:
                                            qa = max(qbs[0], b - 1)
                                            qz = min(qbs[-1], b + 1 if typeA else b)
                                            n = (qz - qa + 1) * 128
                                            c0 = col0 + (qa - qbs[0]) * 128
                                            e0 = (qa - (b - 1)) * 128
                                            if typeA:
                                                kcol = kbase + 128 * b
                                                vap = vx_sb[0][:, b + 1, 2 * kc:2 * kc + 2, :]
                                            else:
                                                kcol = kbase + 128 * b - 64
                                                vap = vx_sb[g][:, r, b, :, :]
                                            jobs.append((r, qa, n, c0, e0, kcol, vap))
                                    for ji, (r, qa, n, c0, e0, kcol, vap) in enumerate(jobs):
                                        first = ji == 0
                                        last = ji == len(jobs) - 1
                                        sp_ = ps_S.next()
                                        for half in range(2):
                                            qv = qv2[half]
                                            S.op("pe", lambda e, sp_=sp_, half=half, r=r, kcol=kcol, qa=qa, n=n, kv=kv, qv=qv: e.matmul(
                                                PSP[sp_][:, half * 512:half * 512 + n], lhsT=kv[:, r, kcol:kcol + 128], rhs=qv[:, r, qa * 128:qa * 128 + n],
                                                start=True, stop=True),
                                                reads=[B_kT[kc], B_qsb[qi]], writes=[B_PS[2 * sp_ + half]])
                                        pi = pe_ring.next()
                                        S.op("act", lambda e, sp_=sp_, pi=pi, n=n: e.activation(
                                            out=pexp[pi][:, :, 0:n], in_=PSP[sp_][:, :].rearrange("p (b c) -> p b c", b=2)[:, :, 0:n], func=AF.Exp, scale=0.125),
                                            reads=[B_PS[2 * sp_], B_PS[2 * sp_ + 1]], writes=[B_pexp[pi]])
                                        ti = pt_ring.next()
                                        S.op(mul_ring.next(), lambda e, pi=pi, ti=ti, n=n, ha=ha, hb=hb, e0=e0: e.tensor_tensor(
                                            out=Pt[ti][:, :, 0:n], in0=pexp[pi][:, :, 0:n], in1=Et[:, ha:hb + 1:hb - ha, e0:e0 + n], op=ALU.mult),
                                            reads=[B_pexp[pi], B_E], writes=[B_Pt[ti]])

                                        def pv(oa=oa, ob=ob, c0=c0, n=n, vap=vap, ti=ti, first=first, last=last, g=g):
                                            for half, po in enumerate((oa, ob)):
                                                S.op("pe", lambda e, half=half, po=po: e.matmul(
                                                    PS[po][:, c0:c0 + n], lhsT=vap[:, half, :], rhs=Pt[ti][:, half, 0:n], start=first, stop=last,
                                                    skip_group_check=True),
                                                    reads=[B_vx[g], B_Pt[ti]], writes=[B_PS[po]])
                                        defer(pv, True)
                                        tick()

                                    def seg_end(oa=oa, ob=ob, mi=mi, dil=dil, cls=list(cls), us=list(us), ha=ha, hb=hb, ai=ai):
                                        for half, po in enumerate((oa, ob)):
                                            if typeA:
                                                up = slice(half * 64, half * 64 + 64)
                                                dp = slice(64 - half * 64, 128 - half * 64)
                                                hh = (ha, hb)[half]
                                                q0 = us[0][1] * 128
                                                ri = rd_ring.next()
                                                S.op("act", lambda e, ri=ri, dp=dp, po=po, hh=hh: e.activation(out=rd[ri][dp, :], in_=PS[po][dp, :], func=AF.Ln,
                                                                                   bias=esink[dp, hh:hh + 1], scale=1.0),
                                                     reads=[B_PS[po], B_const], writes=[B_rd[ri]])
                                                S.op("act", lambda e, ri=ri, dp=dp: e.activation(out=rd[ri][dp, :], in_=rd[ri][dp, :], func=AF.Exp, scale=-1.0),
                                                     reads=[B_rd[ri]], writes=[B_rd[ri]])
                                                S.op("dve", lambda e, ai=ai, up=up, q0=q0, po=po, ri=ri, dp=dp: e.tensor_tensor(out=aout[ai][up, q0:q0 + 512], in0=PS[po][up, :], in1=rd[ri][dp, :], op=ALU.mult),
                                                     reads=[B_PS[po], B_rd[ri]], writes=[B_aout[ai]])
                                                continue
                                            stv = stage[mi][half][:].rearrange("p (i r) -> p r i", r=dil)
                                            if len(cls) == 1:
                                                r = cls[0]
                                                q0 = us[0][1] * 128
                                                S.op("dve", lambda e, stv=stv, r=r, q0=q0, po=po: e.tensor_copy(out=stv[:, r, q0:q0 + 512], in_=PS[po][:, :]),
                                                     reads=[B_PS[po]], writes=[B_stage[mi][half]])
                                            else:
                                                r0 = cls[0]
                                                S.op("dve", lambda e, stv=stv, r0=r0, po=po: e.tensor_copy(
                                                    out=stv[:, r0:r0 + 4, 0:128], in_=PS[po][:, :].rearrange("p (u i) -> p u i", i=128)),
                                                    reads=[B_PS[po]], writes=[B_stage[mi][half]])
                                    defer(seg_end, False)

                            def combine(cset=cset, outs=list(outs), T0=T0, tl=list(tl)):
                                if not typeA:
                                    for half in range(2):
                                        up = slice(half * 64, half * 64 + 64)
                                        dp = slice(64 - half * 64, 128 - half * 64)
                                        S.op("pool", lambda e, half=half, dp=dp: e.tensor_tensor(
                                            out=dsum[half][dp, :], in0=stage[0][half][dp, :], in1=stage[1][half][dp, :], op=ALU.add),
                                            reads=[B_stage[0][half], B_stage[1][half]], writes=[B_dsum[half]])
                                        S.op("pool", lambda e, half=half, dp=dp: e.tensor_tensor(
                                            out=dsum[half][dp, :], in0=dsum[half][dp, :], in1=stage[2][half][dp, :], op=ALU.add),
                                            reads=[B_dsum[half], B_stage[2][half]], writes=[B_dsum[half]])
                                        S.op("act", lambda e, half=half, dp=dp: e.activation(out=dsum[half][dp, :], in_=dsum[half][dp, :], func=AF.Ln),
                                             reads=[B_dsum[half]], writes=[B_dsum[half]])
                                        S.op("act", lambda e, half=half, dp=dp: e.activation(out=dsum[half][dp, :], in_=dsum[half][dp, :], func=AF.Exp, scale=-1.0),
                                             reads=[B_dsum[half]], writes=[B_dsum[half]])
                                        S.op("pool", lambda e, half=half, dp=dp, up=up: e.dma_start(out=dsh[up, :], in_=dsum[half][dp, :]),
                                             reads=[B_dsum[half]], writes=[B_dsh], dma="dsh")
                                        for mi, m in enumerate(cset):
                                            ai = outs[mi]
                                            S.op("dve", lambda e, half=half, up=up, mi=mi, ai=ai: e.tensor_tensor(
                                                out=aout[ai][up, :], in0=stage[mi][half][up, :], in1=dsh[up, :], op=ALU.mult),
                                                reads=[B_stage[mi][half], B_dsh], writes=[B_aout[ai]])
                                for mi, m in enumerate(cset):
                                    ai = outs[mi]
                                    S.op("pool", lambda e, ai=ai, m=m, T0=T0: e.dma_start(out=aT_d[m, :, T0:T0 + ST], in_=aout[ai][:]),
                                         reads=[B_aout[ai]], writes=[B_a[t] for t in tl], dma="aout%d" % ai)
                            defer(combine, False)
                        drain()
                while conv_steps:
                    conv_steps.pop(0)()
                S.flush(final=(stop_after == (l, 2)))
            if stop_after == (l, 2):
                return nc

            with contextlib.ExitStack() as st:
                wsl = [sb(st, "wsl%d" % i, [128, WCAP], BF16) for i in range(4)]
                B_wsl = bufs(4, "wsl")
                aT2 = [sb(st, "aT%d" % i, [128, nq, TT], BF16) for i in range(2)]
                B_aT2 = bufs(2, "aTsb")
                xs2 = [sb(st, "xr%d" % i, [128, 8, TT], F32) for i in range(2)]
                B_xs2 = [bufs(8, "xr%d_" % i) for i in range(2)]
                pin2 = [sb(st, "pin%d" % i, [128, 4, PLE], F32) for i in range(2)]
                B_pin2 = bufs(2, "pin")
                hn = sb(st, "hn", [128, 8, TT], BF16)
                B_hn = bufs(8, "hn_")
                sq = sb(st, "sq3", [128, 8, TT], BF16)
                B_sq = Buf("sq3")
                B_sqc3 = bufs(8, "sqc3")
                lnv = sb(st, "lnv3", [128, TT], F32)
                B_lnv = Buf("lnv3")
                hh_t = sb(st, "hh", [128, NJ, TT], BF16)
                B_hh = bufs(NJ, "hh_")
                sg = [sb(st, "sg%d" % i, [128, TT], F32) for i in range(2)]
                B_sg = bufs(2, "sg")
                t2 = [sb(st, "t2%d" % i, [128, TT], F32) for i in range(2)]
                B_t2 = bufs(2, "t2")
                pT = sb(st, "pT", [128, 2, TT], BF16)
                B_pT = bufs(2, "pT")
                if l == 1:
                    yo2 = [sb(st, "yo%d" % i, [128, 4, D], F32) for i in range(2)]
                    B_yo2 = [bufs(4, "yo%d_" % i) for i in range(2)]
                dmy = sb(st, "dmy", [128, 2], F32)
                B_dmy = Buf("dmy")
                S.op("pool", lambda e: e.memset(dmy[:], 1.0), writes=[B_dmy])

                def preload_ln():
                    S.op("act", lambda e: e.activation(out=dmy[:, 1:2], in_=dmy[:, 0:1], func=AF.Ln), reads=[B_dmy], writes=[B_dmy])
                blks = wblocks(l, cfg)[:-1]
                wpp = sb(st, "wpp", [128, 2048], BF16)
                B_wpp = Buf("wpp")
                S.op("sp", lambda e: e.dma_start(out=wpp[:], in_=wsc[l * NBLK + NBLK - 1, :, 0:2048]),
                     reads=[B_wsc[l * NBLK + NBLK - 1]], writes=[B_wpp], dma="wpp")
                wring = Ring([0, 1, 2, 3])
                ps_r = Ring([0, 1, 2, 3, 4, 5, 7])
                wq = []

                def w_load(bi):
                    slot = wring.next()
                    kind, idx = blks[bi]
                    ne = {"wo": nq * 512, "gu": 16 * 256, "dn": NJ * 256, "pg": 8 * 512, "pp": 2 * 1024}[kind]
                    gb = l * NBLK + bi
                    S.op("sp", lambda e, slot=slot, gb=gb, ne=ne: e.dma_start(out=wsl[slot][:, 0:ne], in_=wsc[gb, :, 0:ne]),
                         reads=[B_wsc[gb]], writes=[B_wsl[slot]], dma="wsl%d" % slot)
                    wq.append(slot)

                def p3_loads(t):
                    slot = t % 2
                    t0 = t * TT
                    S.op("sp", lambda e: e.dma_start(out=aT2[slot][:], in_=aT_d[0:nq, :, t0:t0 + TT].rearrange("c p t -> p c t")),
                         reads=[B_a[t]], writes=[B_aT2[slot]], dma="aTsb%d" % slot)
                    S.op("sp", lambda e: e.dma_start(out=xs2[slot][:], in_=xin_d[:, :, t0:t0 + TT].rearrange("c p t -> p c t")),
                         reads=[B_xin[t]], writes=B_xs2[slot], dma="xr%d" % slot)
                    S.op("sp", lambda e: e.dma_start(out=pin2[slot][:], in_=p_d[l, t0:t0 + TT, :].rearrange("(c p) d -> p c d", p=128)),
                         writes=[B_pin2[slot]], dma="pin%d" % slot)

                seqn = [(t, bi) for t in range(ntile) for bi in range(len(blks))]
                wpos = [0]

                def w_prefetch(upto):
                    while wpos[0] < min(upto, len(seqn)):
                        w_load(seqn[wpos[0]][1])
                        wpos[0] += 1

                used = [0]
                pend_out = []

                def w_next():
                    w_prefetch(used[0] + 4)
                    slot = wq[used[0]]
                    used[0] += 1
                    return slot

                def p3_compute(t, mid_hook):
                    slot = t % 2
                    t0 = t * TT
                    xs, Bx = xs2[slot], B_xs2[slot]
                    aT = aT2[slot]
                    pin = pin2[slot]
                    pendn = []
                    for h2 in range(2):
                        ws = w_next()
                        wv_ = wsl[ws][:, 0:nq * 512].rearrange("p (c n) -> p c n", n=512)
                        for m4 in range(4):
                            m = h2 * 4 + m4
                            pi = ps_r.next()
                            for kc in range(nq):
                                S.op("pe", lambda e, wv_=wv_, kc=kc, m4=m4, pi=pi: e.matmul(PS[pi][:, :], lhsT=wv_[:, kc, m4 * 128:(m4 + 1) * 128], rhs=aT[:, kc, :],
                                                                                           start=(kc == 0), stop=(kc == nq - 1)),
                                     reads=[B_wsl[ws], B_aT2[slot]], writes=[B_PS[pi]])
                            S.op("dve", lambda e, m=m, pi=pi: e.tensor_tensor(out=xs[:, m, :], in0=xs[:, m, :], in1=PS[pi][:, :], op=ALU.add),
                                 reads=[Bx[m], B_PS[pi]], writes=[Bx[m]])
                            norm_sq(xs, Bx, sq, B_sqc3, m)
                            pendn.append(m)
                            if len(pendn) > 1:
                                norm_mm(sq, B_sqc3, 6, pendn.pop(0))
                    while pendn:
                        norm_mm(sq, B_sqc3, 6, pendn.pop(0))
                    while pend_out:
                        pend_out.pop(0)()
                    norm_fin(xs, Bx, gffn[:, l, :], hn, B_hn, lnv, B_lnv, 6)
                    for jj in range(NJ // 2):
                        ws = w_next()
                        if "ffn" in DBG_SKIP:
                            continue
                        wv_ = wsl[ws][:, 0:16 * 256].rearrange("p (c n) -> p c n", n=256)
                        if jj == 0:
                            bk = [ps_r.next() for _ in range(4)]
                            for kc in range(8):
                                for gi, (j2, isup) in enumerate(((0, 0), (0, 1), (1, 0), (1, 1))):
                                    S.op("pe", lambda e, wv_=wv_, kc=kc, j2=j2, isup=isup, pb=bk[gi]: e.matmul(
                                        PS[pb][:, :], lhsT=wv_[:, 8 * isup + kc, j2 * 128:(j2 + 1) * 128], rhs=hn[:, kc, :],
                                        start=(kc == 0), stop=(kc == 7)),
                                        reads=[B_wsl[ws], B_hn[kc]], writes=[B_PS[bk[gi]]])
                            for j2 in range(2):
                                j = j2
                                pg_, pu_ = bk[2 * j2], bk[2 * j2 + 1]
                                si = j % 2
                                S.op("act", lambda e, si=si, pg_=pg_: e.activation(out=sg[si][:], in_=PS[pg_][:, :], func=AF.Silu),
                                     reads=[B_PS[pg_]], writes=[B_sg[si]])
                                S.op("dve", lambda e, si=si, pu_=pu_, j=j: e.tensor_tensor(out=hh_t[:, j, :], in0=sg[si][:], in1=PS[pu_][:, :], op=ALU.mult),
                                     reads=[B_sg[si], B_PS[pu_]], writes=[B_hh[j]])
                            continue
                        for j2 in range(2):
                            j = jj * 2 + j2
                            pg_ = ps_r.next()
                            for kc in range(8):
                                S.op("pe", lambda e, wv_=wv_, kc=kc, j2=j2, pg_=pg_: e.matmul(PS[pg_][:, :], lhsT=wv_[:, kc, j2 * 128:(j2 + 1) * 128], rhs=hn[:, kc, :],
                                                                                             start=(kc == 0), stop=(kc == 7)),
                                     reads=[B_wsl[ws], B_hn[kc]], writes=[B_PS[pg_]])
                            pu_ = ps_r.next()
                            for kc in range(8):
                                S.op("pe", lambda e, wv_=wv_, kc=kc, j2=j2, pu_=pu_: e.matmul(PS[pu_][:, :], lhsT=wv_[:, 8 + kc, j2 * 128:(j2 + 1) * 128], rhs=hn[:, kc, :],
                                                                                             start=(kc == 0), stop=(kc == 7)),
                                     reads=[B_wsl[ws], B_hn[kc]], writes=[B_PS[pu_]])
                            si = j % 2
                            S.op("act", lambda e, si=si, pg_=pg_: e.activation(out=sg[si][:], in_=PS[pg_][:, :], func=AF.Silu),
                                 reads=[B_PS[pg_]], writes=[B_sg[si]])
                            S.op("dve", lambda e, si=si, pu_=pu_, j=j: e.tensor_tensor(out=hh_t[:, j, :], in0=sg[si][:], in1=PS[pu_][:, :], op=ALU.mult),
                                 reads=[B_sg[si], B_PS[pu_]], writes=[B_hh[j]])
                    preload_ln()
                    mid_hook()
                    for mp in range(4):
                        ws = w_next()
                        if "ffn" in DBG_SKIP:
                            continue
                        wv_ = wsl[ws][:, 0:NJ * 256].rearrange("p (c n) -> p c n", n=256)
                        for m2 in range(2):
                            m = mp * 2 + m2
                            pi = ps_r.next()
                            for j in range(NJ):
                                S.op("pe", lambda e, wv_=wv_, j=j, m2=m2, pi=pi: e.matmul(PS[pi][:, :], lhsT=wv_[:, j, m2 * 128:(m2 + 1) * 128], rhs=hh_t[:, j, :],
                                                                                         start=(j == 0), stop=(j == NJ - 1)),
                                     reads=[B_wsl[ws], B_hh[j]], writes=[B_PS[pi]])
                            S.op("dve", lambda e, m=m, pi=pi: e.tensor_tensor(out=xs[:, m, :], in0=xs[:, m, :], in1=PS[pi][:, :], op=ALU.add),
                                 reads=[Bx[m], B_PS[pi]], writes=[Bx[m]])
                            if "ffn" not in DBG_SKIP:
                                norm_sq(xs, Bx, sq, B_sqc3, m)
                                pendn.append(m)
                                if len(pendn) > 1:
                                    norm_mm(sq, B_sqc3, 6, pendn.pop(0))
                    while pendn:
                        norm_mm(sq, B_sqc3, 6, pendn.pop(0))
                    for k2 in range(2):
                        pi = ps_r.next()
                        for tc in range(4):
                            S.op("pe", lambda e, k2=k2, tc=tc, pi=pi: e.transpose(out=PS[pi][:, tc * 128:(tc + 1) * 128], in_=pin[:, tc, k2 * 128:(k2 + 1) * 128],
                                                                                identity=ident[:]),
                                 reads=[B_pin2[slot], B_const], writes=[B_PS[pi]])
                        S.op("act", lambda e, k2=k2, pi=pi: e.copy(out=pT[:, k2, :], in_=PS[pi][:, :]), reads=[B_PS[pi]], writes=[B_pT[k2]])
                    if "ffn" in DBG_SKIP:
                        rmsnorm(xs, Bx, gple[:, l, :], hn, B_hn, sq, B_sq, lnv, B_lnv, 6)
                    else:
                        norm_fin(xs, Bx, gple[:, l, :], hn, B_hn, lnv, B_lnv, 6)
                    wpv = wpp[:, 0:2048].rearrange("p (c n) -> p c n", n=1024)
                    ws_g = None
                    for m in range(8):
                        if m % 4 == 0:
                            ws_g = w_next()
                        if "ple" in DBG_SKIP:
                            continue
                        wgv = wsl[ws_g][:, 0:8 * 512].rearrange("p (c n) -> p c n", n=512)
                        m4 = m % 4
                        pg_ = ps_r.next()
                        for kc in range(8):
                            S.op("pe", lambda e, wgv=wgv, kc=kc, m4=m4, pg_=pg_: e.matmul(PS[pg_][:, :], lhsT=wgv[:, kc, m4 * 128:(m4 + 1) * 128], rhs=hn[:, kc, :],
                                                                                         start=(kc == 0), stop=(kc == 7)),
                                 reads=[B_wsl[ws_g], B_hn[kc]], writes=[B_PS[pg_]])
                        pp_ = ps_r.next()
                        for k2 in range(2):
                            S.op("pe", lambda e, k2=k2, m=m, pp_=pp_: e.matmul(PS[pp_][:, :], lhsT=wpv[:, k2, m * 128:(m + 1) * 128], rhs=pT[:, k2, :],
                                                                               start=(k2 == 0), stop=(k2 == 1)),
                                 reads=[B_wpp, B_pT[k2]], writes=[B_PS[pp_]])
                        si = m % 2
                        S.op("act", lambda e, si=si, pg_=pg_: e.activation(out=sg[si][:], in_=PS[pg_][:, :], func=AF.Sigmoid),
                             reads=[B_PS[pg_]], writes=[B_sg[si]])
                        S.op("dve", lambda e, si=si, pp_=pp_: e.tensor_tensor(out=t2[si][:], in0=sg[si][:], in1=PS[pp_][:, :], op=ALU.mult),
                             reads=[B_sg[si], B_PS[pp_]], writes=[B_t2[si]])
                        S.op("dve", lambda e, si=si, m=m: e.tensor_tensor(out=xs[:, m, :], in0=xs[:, m, :], in1=t2[si][:], op=ALU.add),
                             reads=[Bx[m], B_t2[si]], writes=[Bx[m]])
                    preload_ln()
                    if l == 0:
                        S.op("pool", lambda e: e.dma_start(out=x1T[:, :, t0:t0 + TT].rearrange("c p t -> p c t"), in_=xs[:]),
                             reads=Bx, writes=[B_x1T[t]], dma="xr_st%d" % slot)
                    else:
                        pend_out.append(lambda: p3_output(t))

                def p3_output(t):
                    slot = t % 2
                    t0 = t * TT
                    xs, Bx = xs2[slot], B_xs2[slot]
                    if True:
                        yo, By = yo2[slot], B_yo2[slot]
                        for tc in range(4):
                            for h2 in range(2):
                                pi = ps_r.next()
                                for c4 in range(4):
                                    c = h2 * 4 + c4
                                    S.op("pe", lambda e, c=c, c4=c4, tc=tc, pi=pi: e.transpose(out=PS[pi][:, c4 * 128:(c4 + 1) * 128],
                                                                                               in_=xs[:, c, tc * 128:(tc + 1) * 128], identity=ident[:]),
                                         reads=[Bx[c], B_const], writes=[B_PS[pi]])
                                if h2 == 0:
                                    S.op("act", lambda e, tc=tc, h2=h2, pi=pi: e.copy(out=yo[:, tc, h2 * 512:(h2 + 1) * 512], in_=PS[pi][:, :]),
                                         reads=[B_PS[pi]], writes=[By[tc]])
                                else:
                                    S.op("dve", lambda e, tc=tc, h2=h2, pi=pi: e.tensor_copy(out=yo[:, tc, h2 * 512:(h2 + 1) * 512], in_=PS[pi][:, :]),
                                         reads=[B_PS[pi]], writes=[By[tc]])
                        S.op("pool", lambda e: e.dma_start(out=y_d[t0:t0 + TT, :].rearrange("(c p) d -> p c d", p=128), in_=yo[:]),
                             reads=By, writes=[B_y[t]], dma="yo%d" % slot)

                conv_steps = make_conv_steps(1, st) if l == 0 else []
                per_tile = -(-len(conv_steps) // ntile) if conv_steps else 0
                p3_loads(0)
                for t in range(ntile):
                    def mid_hook(t=t):
                        if t + 1 < ntile:
                            p3_loads(t + 1)
                        for _ in range(per_tile):
                            if conv_steps:
                                conv_steps.pop(0)()
                    p3_compute(t, mid_hook)
                while pend_out:
                    pend_out.pop(0)()
                while conv_steps:
                    conv_steps.pop(0)()
                S.flush(final=(l == 1 or stop_after == (l, 3)))
            if stop_after == (l, 3):
                return nc
        print("total ops", S.total, flush=True)
    return nc


_CACHE = {}


def kernel(x_prompt, x_sample, p_prompt, p_sample, norm_mix, norm_ffn, norm_ple,
           a_wqkv, a_wo, a_q_gain, a_k_gain, a_sink,
           b_wqkv, b_wo, b_q_gain, b_k_gain,
           ffn_w_gate, ffn_w_up, ffn_w_down, ple_w_gate, ple_w_proj):
    f = lambda a: np.ascontiguousarray(np.asarray(a, dtype=np.float32))
    x_prompt, x_sample, p_prompt, p_sample = f(x_prompt), f(x_sample), f(p_prompt), f(p_sample)
    nb, L1, _ = x_prompt.shape
    ns, L2, _ = x_sample.shape
    assert nb == N_CORES and ns == 2 * N_CORES
    seq_lens = [L1, L2, L2]
    key = tuple(seq_lens)
    if key not in _CACHE:
        _CACHE[key] = build(seq_lens)
    nc = _CACHE[key]
    shared = {
        "norm_mix": f(norm_mix), "norm_ffn": f(norm_ffn), "norm_ple": f(norm_ple),
        "a_wqkv": f(a_wqkv)[0], "a_wo": f(a_wo)[0], "a_q_gain": f(a_q_gain)[0], "a_k_gain": f(a_k_gain)[0],
        "a_sink": f(a_sink)[0], "b_wqkv": f(b_wqkv)[0], "b_wo": f(b_wo)[0], "b_q_gain": f(b_q_gain)[0],
        "b_k_gain": f(b_k_gain)[0], "ffn_w_gate": f(ffn_w_gate), "ffn_w_up": f(ffn_w_up),
        "ffn_w_down": f(ffn_w_down), "ple_w_gate": f(ple_w_gate), "ple_w_proj": f(ple_w_proj),
    }
    in_maps = []
    for c in range(N_CORES):
        xc = np.concatenate([x_prompt[c], x_sample[2 * c], x_sample[2 * c + 1]], axis=0)
        pc = np.concatenate([p_prompt[:, c], p_sample[:, 2 * c], p_sample[:, 2 * c + 1]], axis=1)
        m = dict(shared)
        m["x"] = np.ascontiguousarray(xc)
        m["p"] = np.ascontiguousarray(pc)
        in_maps.append(m)
    res = run_bass_kernel_spmd(nc, in_maps, core_ids=list(range(N_CORES)))
    y_prompt = np.empty((nb, L1, D), np.float32)
    y_sample = np.empty((ns, L2, D), np.float32)
    for c in range(N_CORES):
        y = res.results[c]["y"]
        y_prompt[c] = y[0:L1]
        y_sample[2 * c] = y[L1:L1 + L2]
        y_sample[2 * c + 1] = y[L1 + L2:L1 + 2 * L2]
    return (y_prompt, y_sample)
```
